# Optimizing a Trainium2 kernel written in Bass

```python
import math
import jax, jax.numpy as jnp
from jax import lax
import numpy as np

D_MODEL = 1024
BATCH = 4
SEQ = 4096
DEPTH = 2
DEC_BATCH = 128
DEC_SEQ = 8
PAST_LEN = 16384
PAGE_SIZE = 128

HEAD_DIM = 64
N_HEADS_A = 8
DILATED_PATTERNS = ((128, 1), (512, 4), (2048, 16))
WIN_A = 2048
N_HEADS_B = 8
N_KV_B = 2
GQA_GROUP = N_HEADS_B // N_KV_B
WIN_B = 128
BLOCK = 128
ROPE_THETA = 10000.0
D_A = N_HEADS_A * HEAD_DIM
D_BQ = N_HEADS_B * HEAD_DIM
D_BKV = N_KV_B * HEAD_DIM
D_IN_ATTN = 3 * D_A + D_BQ + 2 * D_BKV
D_ATTN_OUT = D_A + D_BQ
D_C = D_MODEL
SSM_GROUP = 16
N_SSM_GROUPS = D_C // SSM_GROUP
SSM_STATE = 64
D_FF = ((8 * D_MODEL // 3 + 255) // 256) * 256
N_ATTN_LAYERS = (DEPTH + 1) // 2
N_SSM_LAYERS = DEPTH // 2
DN_ALPHA = (2 * DEPTH) ** 0.25
DN_BETA = (8 * DEPTH) ** -0.25
FFN_RES = 0.5
LN_EPS = 1e-5
ATTN_SCALE = HEAD_DIM ** -0.5
DT_MIN = 1e-3
DT_MAX = 1e-1

kernel_name = 'hybrid_dilated_swa_s5_decode_step'


def layer_norm(x, g, b):
    xf = x.astype(jnp.float32)
    mu = jnp.mean(xf, -1, keepdims=True)
    var = jnp.mean(jnp.square(xf - mu), -1, keepdims=True)
    return ((xf - mu) * lax.rsqrt(var + LN_EPS) * g + b).astype(x.dtype)


def swiglu(x, w_gate, w_up, w_down):
    return (jax.nn.silu(x @ w_gate) * (x @ w_up)) @ w_down


def ffn_residual(x, w_gate, w_up, w_down, g, b):
    return layer_norm(DN_ALPHA * x + FFN_RES * swiglu(x, w_gate, w_up, w_down), g, b)


def rope(x, pos):
    half = HEAD_DIM // 2
    inv_freq = ROPE_THETA ** (-jnp.arange(half, dtype=jnp.float32) / half)
    ang = pos.astype(jnp.float32)[:, None] * inv_freq[None, :]
    cos = jnp.cos(ang)[None, :, None, :]
    sin = jnp.sin(ang)[None, :, None, :]
    xf = x.astype(jnp.float32)
    x1, x2 = xf[..., :half], xf[..., half:]
    return jnp.concatenate([x1 * cos - x2 * sin, x2 * cos + x1 * sin], -1).astype(x.dtype)


def attn_project(h, pos, w_in):
    bsz, L, _ = h.shape
    z = h @ w_in
    cuts = [D_A, 2 * D_A, 3 * D_A, 3 * D_A + D_BQ, 3 * D_A + D_BQ + D_BKV]
    qa, ka, va, qb, kb, vb = jnp.split(z, cuts, axis=-1)
    qa = rope(qa.reshape(bsz, L, N_HEADS_A, HEAD_DIM), pos)
    ka = rope(ka.reshape(bsz, L, N_HEADS_A, HEAD_DIM), pos)
    va = va.reshape(bsz, L, N_HEADS_A, HEAD_DIM)
    qb = rope(qb.reshape(bsz, L, N_HEADS_B, HEAD_DIM), pos).reshape(bsz, L, N_KV_B, GQA_GROUP, HEAD_DIM)
    kb = rope(kb.reshape(bsz, L, N_KV_B, HEAD_DIM), pos)
    vb = vb.reshape(bsz, L, N_KV_B, HEAD_DIM)
    return qa, ka, va, qb, kb, vb


def dilated_mixture_attend(q, k_all, v_all, q_idx):
    outs, lses = [], []
    for window, dil in DILATED_PATTERNS:
        steps = jnp.arange(window // dil + 1) * dil
        idx = q_idx[:, None] - steps[None, :]
        valid = idx >= 0
        idx = jnp.maximum(idx, 0)
        kg = jnp.take(k_all, idx, axis=1)
        vg = jnp.take(v_all, idx, axis=1)
        s = jnp.einsum('bqhd,bqjhd->bhqj', q, kg).astype(jnp.float32) * ATTN_SCALE
        s = jnp.where(valid[None, None], s, -jnp.inf)
        m = jnp.max(s, -1, keepdims=True)
        p = jnp.exp(s - m)
        den = jnp.sum(p, -1, keepdims=True)
        o = jnp.einsum('bhqj,bqjhd->bqhd', p / den, vg.astype(jnp.float32))
        lses.append(jnp.transpose((m + jnp.log(den))[..., 0], (0, 2, 1)))
        outs.append(o)
    wts = jax.nn.softmax(jnp.stack(lses, 0), axis=0)[..., None]
    return jnp.sum(wts * jnp.stack(outs, 0), 0).astype(q.dtype)


def sink_attend(q, k, v, mask, sinks):
    s = jnp.einsum('bnqkgd,bnskd->bnkgqs', q, k).astype(jnp.float32) * ATTN_SCALE
    s = jnp.where(mask[None, :, None, None], s, -jnp.inf)
    sink = sinks.astype(jnp.float32).reshape(N_KV_B, GQA_GROUP)[None, None, :, :, None, None]
    m = jnp.maximum(jnp.max(s, -1, keepdims=True), sink)
    p = jnp.exp(s - m)
    den = jnp.sum(p, -1, keepdims=True) + jnp.exp(sink - m)
    o = jnp.einsum('bnkgqs,bnskd->bnqkgd', p / den, v.astype(jnp.float32))
    return o.astype(q.dtype)


def band_blocks(x):
    bsz, L = x.shape[:2]
    xb = x.reshape(bsz, L // BLOCK, BLOCK, *x.shape[2:])
    prev = jnp.pad(xb[:, :-1], ((0, 0), (1, 0), (0, 0), (0, 0), (0, 0)))
    return jnp.concatenate([prev, xb], axis=2)


def band_mask(n_blocks):
    blk = jnp.arange(n_blocks)[:, None, None]
    qpos = blk * BLOCK + jnp.arange(BLOCK)[None, :, None]
    kpos = (blk - 1) * BLOCK + jnp.arange(2 * BLOCK)[None, None, :]
    dist = qpos - kpos
    return (dist >= 0) & (dist < WIN_B) & (kpos >= 0)


def attn_mixer_prompt(h, w_in, sinks, w_out):
    bsz, L, _ = h.shape
    nb = L // BLOCK
    qa, ka, va, qb, kb, vb = attn_project(h, jnp.arange(L), w_in)
    q_blocks = jnp.moveaxis(qa.reshape(bsz, nb, BLOCK, N_HEADS_A, HEAD_DIM), 1, 0)
    starts = jnp.arange(nb) * BLOCK
    oa = lax.map(lambda qs: dilated_mixture_attend(qs[0], ka, va, qs[1] + jnp.arange(BLOCK)), (q_blocks, starts))
    oa = jnp.moveaxis(oa, 0, 1).reshape(bsz, L, D_A)
    ob = sink_attend(qb.reshape(bsz, nb, BLOCK, N_KV_B, GQA_GROUP, HEAD_DIM),
                     band_blocks(kb), band_blocks(vb), band_mask(nb), sinks).reshape(bsz, L, D_BQ)
    out = jnp.concatenate([oa, ob], -1) @ w_out
    return out, (ka[:, -WIN_A:], va[:, -WIN_A:], kb[:, -WIN_B:], vb[:, -WIN_B:])


def attn_mixer_sample(h, past_a_k, past_a_v, past_b_k, past_b_v, w_in, sinks, w_out):
    bsz, L, _ = h.shape
    qa, ka, va, qb, kb, vb = attn_project(h, PAST_LEN + jnp.arange(L), w_in)
    n_a = past_a_k.shape[1]
    ka_all = jnp.concatenate([past_a_k, ka], 1)
    va_all = jnp.concatenate([past_a_v, va], 1)
    oa = dilated_mixture_attend(qa, ka_all, va_all, n_a + jnp.arange(L)).reshape(bsz, L, D_A)
    n_b = past_b_k.shape[1]
    kb_all = jnp.concatenate([past_b_k, kb], 1)
    vb_all = jnp.concatenate([past_b_v, vb], 1)
    dist = (n_b + jnp.arange(L))[:, None] - jnp.arange(n_b + L)[None, :]
    mask = ((dist >= 0) & (dist < WIN_B))[None]
    ob = sink_attend(qb[:, None], kb_all[:, None], vb_all[:, None], mask, sinks)[:, 0].reshape(bsz, L, D_BQ)
    out = jnp.concatenate([oa, ob], -1) @ w_out
    return out, (ka, va, kb, vb)


def complex_affine_combine(e1, e2):
    a1r, a1i, b1r, b1i = e1
    a2r, a2i, b2r, b2i = e2
    return (a1r * a2r - a1i * a2i, a1r * a2i + a1i * a2r,
            a2r * b1r - a2i * b1i + b2r, a2r * b1i + a2i * b1r + b2i)


def ssm_discretize(lam_re, lam_im, log_dt, b_re, b_im):
    dt = jnp.exp(log_dt.astype(jnp.float32))[:, None]
    lr, li = lam_re.astype(jnp.float32), lam_im.astype(jnp.float32)
    mag = jnp.exp(lr * dt)
    ab_re, ab_im = mag * jnp.cos(li * dt), mag * jnp.sin(li * dt)
    nr, ni = ab_re - 1.0, ab_im
    den = lr * lr + li * li
    fr, fi = (nr * lr + ni * li) / den, (ni * lr - nr * li) / den
    bb_re = fr[..., None] * b_re - fi[..., None] * b_im
    bb_im = fr[..., None] * b_im + fi[..., None] * b_re
    return ab_re, ab_im, bb_re, bb_im


def ssm_mixer(h, h0_re, h0_im, w_in, lam_re, lam_im, log_dt, b_re, b_im, c_re, c_im, d_skip, w_glu, b_glu, w_out):
    bsz, L, _ = h.shape
    u = (h @ w_in).astype(jnp.float32).reshape(bsz, L, N_SSM_GROUPS, SSM_GROUP)
    ab_re, ab_im, bb_re, bb_im = ssm_discretize(lam_re, lam_im, log_dt, b_re, b_im)
    bu_re = jnp.einsum('blgn,gpn->lbgp', u, bb_re)
    bu_im = jnp.einsum('blgn,gpn->lbgp', u, bb_im)
    bu_re = bu_re.at[0].add(ab_re * h0_re - ab_im * h0_im)
    bu_im = bu_im.at[0].add(ab_re * h0_im + ab_im * h0_re)
    a_re = jnp.broadcast_to(ab_re, (L, 1, N_SSM_GROUPS, SSM_STATE))
    a_im = jnp.broadcast_to(ab_im, (L, 1, N_SSM_GROUPS, SSM_STATE))
    _, _, hr, hi = lax.associative_scan(complex_affine_combine, (a_re, a_im, bu_re, bu_im), axis=0)
    y = jnp.einsum('lbgp,gnp->blgn', hr, c_re) - jnp.einsum('lbgp,gnp->blgn', hi, c_im)
    y = y + d_skip.reshape(N_SSM_GROUPS, SSM_GROUP) * u
    y = jax.nn.gelu(y.reshape(bsz, L, D_C))
    y = y * jax.nn.sigmoid(y @ w_glu + b_glu)
    return y.astype(h.dtype) @ w_out, hr[-1], hi[-1]


def setup_inputs(seed: int = 0) -> dict:
    key = jax.random.key(seed)
    ks = iter(jax.random.split(key, 40))
    nrm = lambda shape, s=1.0: s * jax.random.normal(next(ks), shape, jnp.float32)
    nbuf_a = min(WIN_A, PAST_LEN)
    nbuf_b = min(WIN_B, PAST_LEN)
    G, P, N = N_SSM_GROUPS, SSM_STATE, SSM_GROUP
    NA, NS = N_ATTN_LAYERS, N_SSM_LAYERS
    return {
        'x_prompt': nrm((BATCH, SEQ, D_MODEL)),
        'x_sample': nrm((DEC_BATCH, DEC_SEQ, D_MODEL)),
        'cache_a_k': nrm((NA, DEC_BATCH, nbuf_a, N_HEADS_A, HEAD_DIM)),
        'cache_a_v': nrm((NA, DEC_BATCH, nbuf_a, N_HEADS_A, HEAD_DIM)),
        'cache_b_k': nrm((NA, DEC_BATCH, nbuf_b, N_KV_B, HEAD_DIM)),
        'cache_b_v': nrm((NA, DEC_BATCH, nbuf_b, N_KV_B, HEAD_DIM)),
        'state_c_re': nrm((NS, DEC_BATCH, G, P), 0.1),
        'state_c_im': nrm((NS, DEC_BATCH, G, P), 0.1),
        'ln_g': 1.0 + nrm((DEPTH, 3, D_MODEL), 0.01),
        'ln_b': nrm((DEPTH, 3, D_MODEL), 0.01),
        'ffn_w_gate': nrm((DEPTH, 2, D_MODEL, D_FF), D_MODEL ** -0.5),
        'ffn_w_up': nrm((DEPTH, 2, D_MODEL, D_FF), D_MODEL ** -0.5),
        'ffn_w_down': nrm((DEPTH, 2, D_FF, D_MODEL), DN_BETA * D_FF ** -0.5),
        'attn_w_in': nrm((NA, D_MODEL, D_IN_ATTN), D_MODEL ** -0.5),
        'attn_sinks': nrm((NA, N_HEADS_B), 0.5),
        'attn_w_out': nrm((NA, D_ATTN_OUT, D_MODEL), DN_BETA * D_ATTN_OUT ** -0.5),
        'ssm_w_in': nrm((NS, D_MODEL, D_C), D_MODEL ** -0.5),
        'ssm_lambda_re': -0.5 + nrm((NS, G, P), 0.01),
        'ssm_lambda_im': math.pi * jnp.arange(P, dtype=jnp.float32) + nrm((NS, G, P), 0.01),
        'ssm_log_dt': jax.random.uniform(next(ks), (NS, G), jnp.float32, math.log(DT_MIN), math.log(DT_MAX)),
        'ssm_b_re': nrm((NS, G, P, N), (2 * N) ** -0.5),
        'ssm_b_im': nrm((NS, G, P, N), (2 * N) ** -0.5),
        'ssm_c_re': nrm((NS, G, N, P), (2 * P) ** -0.5),
        'ssm_c_im': nrm((NS, G, N, P), (2 * P) ** -0.5),
        'ssm_d': nrm((NS, D_C)),
        'ssm_w_glu': nrm((NS, D_C, D_C), D_C ** -0.5),
        'ssm_b_glu': nrm((NS, D_C), 0.01),
        'ssm_w_out': nrm((NS, D_C, D_MODEL), DN_BETA * D_C ** -0.5),
    }


def reference(x_prompt, x_sample, cache_a_k, cache_a_v, cache_b_k, cache_b_v, state_c_re, state_c_im,
              ln_g, ln_b, ffn_w_gate, ffn_w_up, ffn_w_down, attn_w_in, attn_sinks, attn_w_out,
              ssm_w_in, ssm_lambda_re, ssm_lambda_im, ssm_log_dt, ssm_b_re, ssm_b_im, ssm_c_re, ssm_c_im,
              ssm_d, ssm_w_glu, ssm_b_glu, ssm_w_out):
    yp, ys = x_prompt, x_sample
    pa_k, pa_v, pb_k, pb_v, pc_re, pc_im = [], [], [], [], [], []
    sa_k, sa_v, sb_k, sb_v, sc_re, sc_im = [], [], [], [], [], []
    for l in range(DEPTH):
        i = l // 2
        yp = ffn_residual(yp, ffn_w_gate[l, 0], ffn_w_up[l, 0], ffn_w_down[l, 0], ln_g[l, 0], ln_b[l, 0])
        ys = ffn_residual(ys, ffn_w_gate[l, 0], ffn_w_up[l, 0], ffn_w_down[l, 0], ln_g[l, 0], ln_b[l, 0])
        if l % 2 == 0:
            mp, (ka, va, kb, vb) = attn_mixer_prompt(yp, attn_w_in[i], attn_sinks[i], attn_w_out[i])
            pa_k.append(ka); pa_v.append(va); pb_k.append(kb); pb_v.append(vb)
            ms, (ka, va, kb, vb) = attn_mixer_sample(ys, cache_a_k[i], cache_a_v[i], cache_b_k[i], cache_b_v[i],
                                                     attn_w_in[i], attn_sinks[i], attn_w_out[i])
            sa_k.append(ka); sa_v.append(va); sb_k.append(kb); sb_v.append(vb)
        else:
            ssm_w = (ssm_w_in[i], ssm_lambda_re[i], ssm_lambda_im[i], ssm_log_dt[i], ssm_b_re[i], ssm_b_im[i],
                     ssm_c_re[i], ssm_c_im[i], ssm_d[i], ssm_w_glu[i], ssm_b_glu[i], ssm_w_out[i])
            h0 = jnp.zeros((yp.shape[0], N_SSM_GROUPS, SSM_STATE), jnp.float32)
            mp, hr, hi = ssm_mixer(yp, h0, h0, *ssm_w)
            pc_re.append(hr); pc_im.append(hi)
            ms, hr, hi = ssm_mixer(ys, state_c_re[i], state_c_im[i], *ssm_w)
            sc_re.append(hr); sc_im.append(hi)
        yp = layer_norm(DN_ALPHA * yp + mp, ln_g[l, 1], ln_b[l, 1])
        ys = layer_norm(DN_ALPHA * ys + ms, ln_g[l, 1], ln_b[l, 1])
        yp = ffn_residual(yp, ffn_w_gate[l, 1], ffn_w_up[l, 1], ffn_w_down[l, 1], ln_g[l, 2], ln_b[l, 2])
        ys = ffn_residual(ys, ffn_w_gate[l, 1], ffn_w_up[l, 1], ffn_w_down[l, 1], ln_g[l, 2], ln_b[l, 2])
    return (yp, ys,
            jnp.stack(pa_k), jnp.stack(pa_v), jnp.stack(pb_k), jnp.stack(pb_v), jnp.stack(pc_re), jnp.stack(pc_im),
            jnp.stack(sa_k), jnp.stack(sa_v), jnp.stack(sb_k), jnp.stack(sb_v), jnp.stack(sc_re), jnp.stack(sc_im))
```

```python
import contextlib
import math
import numpy as np
import concourse.bass as bass
import concourse.mybir as mybir
from concourse.bass_utils import run_bass_kernel_spmd

F32 = mybir.dt.float32
BF16 = mybir.dt.bfloat16
AF = mybir.ActivationFunctionType
ALU = mybir.AluOpType

D = 1024
KT = 8
DFF = 2816
NFF = 22
SEQ = 4096
UT = 512
NUNIT = SEQ // UT
ALPHA = 2.0 ** 0.5
EPS = 1e-5
NSLOT = 3

CFG = {"n_units": NUNIT, "stage": 99, "sample": True}


class LB:
    __slots__ = ("w", "r")

    def __init__(self):
        self.w = None
        self.r = {}


def lbs(n):
    return [LB() for _ in range(n)]


class Prog:
    def __init__(self, nc):
        self.nc = nc
        self.es = contextlib.ExitStack()
        self.h = {"pe": nc.tensor, "act": nc.scalar, "dve": nc.vector, "pool": nc.gpsimd, "sp": nc.sync}
        self.sem = {e: self.es.enter_context(nc.semaphore("E_" + e)) for e in self.h}
        self.cnt = {e: 0 for e in self.h}
        self.waited = {e: {} for e in self.h}
        self.dsem = {}
        self.ninst = 0
        names = ["c", "c2", "xi0", "xi1", "yo0", "yo1", "rp", "ko0", "ko1", "vo0", "vo1", "sc", "sc2", "tws0", "tws1", "twc0", "twc1",
                 "tl0", "tl1", "fo", "ck0", "ck1", "cv0", "cv1", "cb0", "cb1", "vq0", "vq1"] + [f"w{i}" for i in range(NSLOT)]
        for n in names:
            self.dsem[n] = [self.es.enter_context(nc.semaphore("D_" + n)), 0]

    def _wait(self, e, ev):
        sem, val, src = ev
        if src == e and e == "pe":
            return
        k = id(sem)
        if self.waited[e].get(k, 0) >= val:
            return
        self.h[e].wait_ge(sem, val)
        self.waited[e][k] = val

    def _deps(self, e, reads, writes):
        for b in reads:
            if b.w is not None:
                self._wait(e, b.w)
        for b in writes:
            if b.w is not None:
                self._wait(e, b.w)
            for ev in b.r.values():
                self._wait(e, ev)

    def _post(self, ev, reads, writes):
        k = id(ev[0])
        for b in reads:
            o = b.r.get(k)
            if o is None or o[1] < ev[1]:
                b.r[k] = ev
        for b in writes:
            b.w = ev
            b.r = {}

    def op(self, e, fn, reads=(), writes=(), inc=True):
        self._deps(e, reads, writes)
        ins = fn(self.h[e])
        self.ninst += 1
        if inc:
            self.cnt[e] += 1
            ins.then_inc(self.sem[e], 1)
            ev = (self.sem[e], self.cnt[e], e)
        else:
            ev = (self.sem[e], self.cnt[e] + 1, e)
        self._post(ev, reads, writes)
        return ins

    def dma(self, q, out, in_, reads=(), writes=(), sem="d"):
        self._deps(q, reads, writes)
        if sem not in self.dsem:
            self.dsem[sem] = [self.es.enter_context(self.nc.semaphore("D_" + sem)), 0]
        d = self.dsem[sem]
        d[1] += 16
        self.h[q].dma_start(out=out, in_=in_).then_inc(d[0], 16)
        self.ninst += 1
        ev = (d[0], d[1], None)
        self._post(ev, reads, writes)

    def barrier(self):
        for e in self.h:
            for o in self.h:
                if o != e and self.cnt[o] > 0:
                    self._wait(e, (self.sem[o], self.cnt[o], o))
            for d in self.dsem.values():
                if d[1] > 0:
                    self._wait(e, (d[0], d[1], None))


def build(cfg):
    nc = bass.Bass("TRN2", target_bir_lowering=False)
    P = Prog(nc)
    es = P.es
    n_units = cfg["n_units"]
    stage = cfg["stage"]

    def din(name, shape, dt=F32):
        return nc.dram_tensor(name, list(shape), dt, kind="ExternalInput").ap()

    def dout(name, shape, dt=F32):
        return nc.dram_tensor(name, list(shape), dt, kind="ExternalOutput").ap()

    uid = [0]

    def sb(name, shape, dt, stack=es):
        uid[0] += 1
        return stack.enter_context(nc.sbuf_tensor(f"{name}_{uid[0]}", list(shape), dt))

    xpT = din("xpT", [D, SEQ])
    xsT = din("xsT", [D, 128])
    wg = din("wg", [4, D, DFF])
    wu = din("wu", [4, D, DFF])
    wd = din("wd", [4, DFF, D])
    lng = din("lng", [128, 48])
    lnb = din("lnb", [128, 48])
    w_in = din("w_in", [D, 2304])
    w_out = din("w_out", [D, D])
    ropec = din("ropec", [128, SEQ])
    ropes = din("ropes", [128, SEQ])
    ropecs = din("ropecs", [128, 128])
    ropess = din("ropess", [128, 128])
    mA_d = din("mA", [128, 17 * 128])
    mB_d = din("mB", [128, 2 * 128])
    msA_d = din("msA", [128, 17 * 8])
    msB_d = din("msB", [128, 2 * 8])
    sinks_d = din("sinks", [128, 8])
    if cfg["sample"]:
        cakT = din("cakT", [16, 512, 2048])
        cav = din("cav", [16, 2048, 512])
        cbkT = din("cbkT", [16, 128, 128])
        cbv = din("cbv", [16, 128, 128])
    pakT = dout("pakT", [512, 2048])
    pav = dout("pav", [2048, 512])
    pbkT = dout("pbkT", [128, 128])
    pbv = dout("pbv", [128, 128])
    sakT = dout("sakT", [512, 128])
    sav = dout("sav", [128, 512])
    sbkT = dout("sbkT", [128, 128])
    sbv = dout("sbv", [128, 128])
    ypT = dout("ypT", [D, SEQ])
    ysT = dout("ysT", [D, 128])

    x32 = sb("x32", [128, KT, UT], F32)
    xb = sb("xb", [128, KT, UT], BF16)
    x32_lb = lbs(KT)
    xb_lb = lbs(KT)
    wring = [sb(f"wring{i}", [128, 4096], BF16) for i in range(NSLOT)]
    wring_lb = lbs(NSLOT)
    ps = es.enter_context(nc.psum_tensor("ps", [128, 8, 512], F32))
    ps_lb = lbs(8)
    ones_bf = sb("ones_bf", [128, 128], BF16)
    g_sb = sb("g_sb", [128, 48], F32)
    b_sb = sb("b_sb", [128, 48], F32)
    ga_sb = sb("ga_sb", [128, 48], F32)
    ba_sb = sb("ba_sb", [128, 48], F32)
    eps_sb = sb("eps_sb", [128, 1], F32)
    const_lb = LB()
    xsq = sb("xsq", [128, 2, UT], BF16)
    xsq_lb = lbs(2)
    st_mean = sb("st_mean", [128, UT], F32)
    st_a = sb("st_a", [128, UT], F32)
    st_rstd = sb("st_rstd", [128, UT], F32)
    st_lb = lbs(3)
    ttmp = sb("ttmp", [128, 2, UT], F32)
    ttmp_lb = lbs(2)
    ystage = sb("ystage", [128, 2, UT], F32)
    ystage_lb = lbs(2)

    NRB = 20
    KTs = sb("KTs", [128, 4, NRB * 128], BF16)
    Vst = sb("Vst", [128, NRB, 512], BF16)
    kt_lb = lbs(NRB)
    v_lb = lbs(NRB)
    KTB = sb("KTB", [128, 2, 8, 128], BF16)
    VB = sb("VB", [128, 8, 128], BF16)
    ktb_lb = lbs(8)
    vb_lb = lbs(8)
    mA = sb("mA", [128, 17, 128], BF16)
    mB = sb("mB", [128, 2, 128], BF16)
    msA = sb("msA", [128, 17, 8], BF16)
    msB = sb("msB", [128, 2, 8], BF16)
    sink_e = sb("sink_e", [128, 8], F32)
    zrow = sb("zrow", [1, 512], BF16)
    P.dma("pool", mA[:], mA_d.rearrange("p (a b) -> p a b", a=17), writes=[const_lb], sem="c2")
    P.dma("pool", mB[:], mB_d.rearrange("p (a b) -> p a b", a=2), writes=[const_lb], sem="c2")
    P.dma("pool", msA[:], msA_d.rearrange("p (a b) -> p a b", a=17), writes=[const_lb], sem="c2")
    P.dma("pool", msB[:], msB_d.rearrange("p (a b) -> p a b", a=2), writes=[const_lb], sem="c2")
    P.dma("sp", sink_e[:], sinks_d[:, :], writes=[const_lb], sem="c")
    P.op("act", lambda h: h.activation(out=sink_e[:], in_=sink_e[:], func=AF.Exp), reads=[const_lb], writes=[const_lb])
    P.op("dve", lambda h: h.memset(zrow[:], 0.0), writes=[const_lb])

    P.op("dve", lambda h: h.memset(ones_bf[:], 1.0), writes=[const_lb])
    P.op("dve", lambda h: h.memset(eps_sb[:], EPS), writes=[const_lb])
    P.dma("sp", g_sb[:], lng[:, :], writes=[const_lb], sem="c")
    P.dma("sp", b_sb[:], lnb[:, :], writes=[const_lb], sem="c")
    P.op("act", lambda h: h.mul(ga_sb[:], g_sb[:], ALPHA), reads=[const_lb], writes=[const_lb])
    P.op("act", lambda h: h.mul(ba_sb[:], b_sb[:], ALPHA), reads=[const_lb], writes=[const_lb])

    P.barrier()

    class WStream:
        def __init__(self):
            self.items = []
            self.issued = 0
            self.total = 0

        def add(self, dram_ap, shape):
            self.items.append((dram_ap, shape))
            return len(self.items) - 1

        def _issue(self, i):
            dram_ap, shape = self.items[i]
            gi = self.base + i
            s = gi % NSLOT
            n = int(np.prod(shape))
            view = wring[s][:, 0:n]
            if len(shape) == 2:
                view = view.rearrange("p (a b) -> p a b", a=shape[0])
            if dram_ap.shape[0] == 64:
                view = view[0:64]
            P.dma("pool", view, dram_ap, writes=[wring_lb[s]], sem=f"w{s}")

        def start(self, base):
            self.base = base
            self.issued = 0

        def get(self, i):
            while self.issued < len(self.items) and self.issued <= i + NSLOT - 2:
                self._issue(self.issued)
                self.issued += 1
            gi = self.base + i
            s = gi % NSLOT
            _, shape = self.items[i]
            n = int(np.prod(shape))
            view = wring[s][:, 0:n]
            if len(shape) == 2:
                view = view.rearrange("p (a b) -> p a b", a=shape[0])
            return view, wring_lb[s]

    wbase = [0]

    def new_stream():
        w = WStream()
        return w

    def start_stream(w):
        w.start(wbase[0])
        wbase[0] += len(w.items)

    def layer_norm(lj, N, final_out=None):
        inv = 1.0 / D
        s1, s2 = ps[:, 6, :N], ps[:, 7, :N]
        for k in range(KT):
            q = k % 2
            P.op("act", lambda h, k=k, q=q: h.activation(out=xsq[:, q, :N], in_=x32[:, k, :N], func=AF.Square),
                 reads=[x32_lb[k]], writes=[xsq_lb[q]])
            P.op("dve", lambda h, k=k: h.tensor_copy(out=xb[:, k, :N], in_=x32[:, k, :N]),
                 reads=[x32_lb[k]], writes=[xb_lb[k]])
            P.op("pe", lambda h, k=k: h.matmul(s1, lhsT=ones_bf[:], rhs=xb[:, k, :N], start=(k == 0), stop=(k == KT - 1)),
                 reads=[xb_lb[k], const_lb], writes=[ps_lb[6]], inc=(k == KT - 1))
            P.op("pe", lambda h, k=k, q=q: h.matmul(s2, lhsT=ones_bf[:], rhs=xsq[:, q, :N], start=(k == 0), stop=(k == KT - 1)),
                 reads=[xsq_lb[q], const_lb], writes=[ps_lb[7]], inc=True)
        P.op("act", lambda h: h.activation(out=st_mean[:, :N], in_=s1, func=AF.Copy, scale=inv),
             reads=[ps_lb[6]], writes=[st_lb[0]])
        P.op("act", lambda h: h.activation(out=st_a[:, :N], in_=s1, func=AF.Square, scale=inv),
             reads=[ps_lb[6]], writes=[st_lb[1]])
        P.op("dve", lambda h: h.scalar_tensor_tensor(out=st_a[:, :N], in0=s2, scalar=inv, in1=st_a[:, :N],
                                                     op0=ALU.mult, op1=ALU.subtract),
             reads=[ps_lb[7], st_lb[1]], writes=[st_lb[1]])
        P.op("act", lambda h: h.activation(out=st_a[:, :N], in_=st_a[:, :N], func=AF.Sqrt, bias=eps_sb[:, 0:1], scale=1.0),
             reads=[st_lb[1], const_lb], writes=[st_lb[1]])
        P.op("dve", lambda h: h.reciprocal(out=st_rstd[:, :N], in_=st_a[:, :N]),
             reads=[st_lb[1]], writes=[st_lb[2]])
        for k in range(KT):
            q = k % 2
            c = lj * KT + k
            P.op("dve", lambda h, k=k, q=q: h.tensor_tensor(out=ttmp[:, q, :N], in0=x32[:, k, :N], in1=st_mean[:, :N], op=ALU.subtract),
                 reads=[x32_lb[k], st_lb[0]], writes=[ttmp_lb[q]])
            P.op("dve", lambda h, q=q: h.tensor_tensor(out=ttmp[:, q, :N], in0=ttmp[:, q, :N], in1=st_rstd[:, :N], op=ALU.mult),
                 reads=[ttmp_lb[q], st_lb[2]], writes=[ttmp_lb[q]])
            P.op("act", lambda h, k=k, q=q, c=c: h.activation(out=xb[:, k, :N], in_=ttmp[:, q, :N], func=AF.Identity,
                                                               scale=g_sb[:, c:c + 1], bias=b_sb[:, c:c + 1]),
                 reads=[ttmp_lb[q], const_lb], writes=[xb_lb[k]])
            if final_out is None:
                P.op("act", lambda h, k=k, q=q, c=c: h.activation(out=x32[:, k, :N], in_=ttmp[:, q, :N], func=AF.Identity,
                                                                   scale=ga_sb[:, c:c + 1], bias=ba_sb[:, c:c + 1]),
                     reads=[ttmp_lb[q], const_lb], writes=[x32_lb[k]])
            else:
                P.op("act", lambda h, k=k, q=q, c=c: h.activation(out=ystage[:, q, :N], in_=ttmp[:, q, :N], func=AF.Identity,
                                                                   scale=g_sb[:, c:c + 1], bias=b_sb[:, c:c + 1]),
                     reads=[ttmp_lb[q], const_lb], writes=[ystage_lb[q]])
                P.dma("sp", final_out(k), ystage[:, q, :N], reads=[ystage_lb[q]], sem=f"yo{q}")

    def ffn(fi, lj, N, final_out=None):
        with contextlib.ExitStack() as fs:
            hbuf = sb("hbuf", [128, NFF, UT], BF16, fs)
            hbuf_lb = lbs(NFF)
            sil = sb("sil", [128, 2, UT], F32, fs)
            sil_lb = lbs(2)
            wgv = wg[fi].rearrange("(k p) n -> p k n", p=128)
            wuv = wu[fi].rearrange("(k p) n -> p k n", p=128)
            wdv = wd[fi].rearrange("(k p) n -> p k n", p=128)
            ws = new_stream()
            groups = []
            for g in range(6):
                gw = 512 if g < 5 else 256
                ig = ws.add(wgv[:, :, g * 512:g * 512 + gw], (KT, gw))
                iu = ws.add(wuv[:, :, g * 512:g * 512 + gw], (KT, gw))
                groups.append((ig, iu, gw))
            dts = [ws.add(wdv[:, :, o * 128:(o + 1) * 128], (NFF, 128)) for o in range(KT)]
            start_stream(ws)
            for g, (ig, iu, gw) in enumerate(groups):
                sg, sg_lb = ws.get(ig)
                su, su_lb = ws.get(iu)
                for c in range(gw // 128):
                    ff = 4 * g + c
                    q = ff % 2
                    pg, pu = ps[:, q, :N], ps[:, 2 + q, :N]
                    for k in range(KT):
                        P.op("pe", lambda h, k=k, c=c, sg=sg, pg=pg: h.matmul(pg, lhsT=sg[:, k, c * 128:(c + 1) * 128], rhs=xb[:, k, :N],
                                                                 start=(k == 0), stop=(k == KT - 1)),
                             reads=[sg_lb, xb_lb[k]], writes=[ps_lb[q]], inc=(k == KT - 1))
                    for k in range(KT):
                        P.op("pe", lambda h, k=k, c=c, su=su, pu=pu: h.matmul(pu, lhsT=su[:, k, c * 128:(c + 1) * 128], rhs=xb[:, k, :N],
                                                                 start=(k == 0), stop=(k == KT - 1)),
                             reads=[su_lb, xb_lb[k]], writes=[ps_lb[2 + q]], inc=(k == KT - 1))
                    P.op("act", lambda h, q=q, pg=pg: h.activation(out=sil[:, q, :N], in_=pg, func=AF.Silu),
                         reads=[ps_lb[q]], writes=[sil_lb[q]])
                    P.op("dve", lambda h, q=q, ff=ff, pu=pu: h.tensor_tensor(out=hbuf[:, ff, :N], in0=sil[:, q, :N], in1=pu, op=ALU.mult),
                         reads=[sil_lb[q], ps_lb[2 + q]], writes=[hbuf_lb[ff]])
            for o in range(KT):
                sd, sd_lb = ws.get(dts[o])
                b = 4 + (o % 2)
                po = ps[:, b, :N]
                for k in range(NFF):
                    P.op("pe", lambda h, k=k, sd=sd, po=po: h.matmul(po, lhsT=sd[:, k, :], rhs=hbuf[:, k, :N],
                                                                     start=(k == 0), stop=(k == NFF - 1)),
                         reads=[sd_lb, hbuf_lb[k]], writes=[ps_lb[b]], inc=(k == NFF - 1))
                P.op("dve", lambda h, o=o, po=po: h.scalar_tensor_tensor(out=x32[:, o, :N], in0=po, scalar=0.5, in1=x32[:, o, :N],
                                                                         op0=ALU.mult, op1=ALU.add),
                     reads=[ps_lb[b], x32_lb[o]], writes=[x32_lb[o]])
            layer_norm(lj, N, final_out)
            P.barrier()


    SC = 0.125

    def attention(N, t0, is_sample):
        nb = N // 128
        gb0 = t0 // 128
        with contextlib.ExitStack() as fs:
            QA = sb("QA", [128, 8, N], BF16, fs)
            QB = sb("QB", [128, 8, N], BF16, fs)
            qa_lb, qb_lb = LB(), LB()
            P.op("dve", lambda h: h.memset(QA[:], 0.0), writes=[qa_lb])
            P.op("dve", lambda h: h.memset(QB[:], 0.0), writes=[qb_lb])
            wrot = sb("wrot", [128, 4096], BF16, fs)
            wrot_lb = LB()
            wdup = wrot[:, 0:2048].rearrange("p (k v c) -> p k v c", k=KT, v=2)
            wdupr = wrot[:, 2048:4096].rearrange("p (k v c) -> p k v c", k=KT, v=2)
            wdup_lb = wrot_lb
            rc = sb("rc", [128, N], F32, fs)
            rs = sb("rs", [128, N], F32, fs)
            rope_lb = LB()
            r1, r1_lb = ttmp, ttmp_lb
            r2, r2_lb = ystage, ystage_lb
            PT = sb("PT", [128, 2, 512], BF16, fs)
            pt_lb = lbs(2)
            oT = sb("oT", [64, 16, N], BF16, fs)
            ot_lb = LB()
            rd = sb("rd", [64, 512], F32, fs)
            rd_lb = LB()
            vstg, vstg_lb = ystage, ystage_lb
            if is_sample:
                KTn = sb("KTn", [128, 4, 128], BF16, fs)
                KTBn = sb("KTBn", [128, 2, 128], BF16, fs)
                Vn = sb("Vn", [128, 640], BF16, fs)
                Vsq = sb("Vsq", [8, 2, 640], BF16, fs)
                vsq_lb = lbs(2)
                KBs = sb("KBs", [128, 2, 2, 128], BF16, fs)
                VBs = sb("VBs", [128, 2, 128], BF16, fs)
                kbs_lb = lbs(2)
                ktn_lb = LB()
            csrc, ssrc = (ropecs, ropess) if is_sample else (ropec, ropes)
            P.dma("sp", rc[:], csrc[:, t0:t0 + N], writes=[rope_lb], sem="rp")
            P.dma("sp", rs[:], ssrc[:, t0:t0 + N], writes=[rope_lb], sem="rp")
            wv = w_in.rearrange("(k p) n -> p k n", p=128)
            ws = new_stream()
            tiles = [ws.add(wv[:, :, i * 512:(i + 1) * 512], (KT, 512)) for i in range(4)]
            tiles.append(ws.add(wv[:, :, 2048:2304], (KT, 256)))
            wov = w_out.rearrange("(h d) n -> d h n", d=64)
            wo_t = [ws.add(wov[:, :, o * 128:(o + 1) * 128], (16, 128)) for o in range(KT)]
            start_stream(ws)
            cnt = [0]

            def make_rot(src, dst, src_lb):
                sv = src.rearrange("p (a t i) -> p a t i", t=2, i=32)
                dv = dst.rearrange("p (a t i) -> p a t i", t=2, i=32)
                P.op("act", lambda h: h.mul(dv[:, :, 0, :], sv[:, :, 1, :], -1.0), reads=[src_lb], writes=[wrot_lb])
                P.op("dve", lambda h: h.tensor_copy(out=dv[:, :, 1, :], in_=sv[:, :, 0, :]), reads=[src_lb], writes=[wrot_lb])

            def proj_rope(wslot, wslot_lb, rotv, rot_lb, col0, dst_bf, dst_lbs, out32=None, qz=None):
                i = cnt[0] % 2
                cnt[0] += 1
                pz, pr = ps[:, i, :N], ps[:, 2 + i, :N]
                for k in range(KT):
                    P.op("pe", lambda h, k=k: h.matmul(pz, lhsT=wslot[:, k, col0:col0 + 128], rhs=xb[:, k, :N], start=(k == 0), stop=(k == KT - 1)),
                         reads=[wslot_lb, xb_lb[k]], writes=[ps_lb[i]], inc=(k == KT - 1))
                for k in range(KT):
                    P.op("pe", lambda h, k=k: h.matmul(pr, lhsT=rotv[:, k, col0:col0 + 128], rhs=xb[:, k, :N], start=(k == 0), stop=(k == KT - 1)),
                         reads=[rot_lb, xb_lb[k]], writes=[ps_lb[2 + i]], inc=(k == KT - 1))
                P.op("dve", lambda h: h.tensor_tensor(out=r1[:, i, :N], in0=pz, in1=rc[:], op=ALU.mult), reads=[ps_lb[i], rope_lb], writes=[r1_lb[i]])
                P.op("dve", lambda h: h.tensor_tensor(out=r2[:, i, :N], in0=pr, in1=rs[:], op=ALU.mult), reads=[ps_lb[2 + i], rope_lb], writes=[r2_lb[i]])
                a1, a2 = r1[:, i, :N], r2[:, i, :N]
                if len(dst_bf.shape) == 3:
                    a1 = a1.rearrange("p (a b) -> p a b", b=128)
                    a2 = a2.rearrange("p (a b) -> p a b", b=128)
                if qz is not None:
                    Qt, hp_ = qz
                    P.op("dve", lambda h: h.tensor_tensor(out=Qt[0:64, 2 * hp_, :], in0=r1[0:64, i, :N], in1=r2[0:64, i, :N], op=ALU.add), reads=[r1_lb[i], r2_lb[i]], writes=dst_lbs)
                    P.op("dve", lambda h: h.tensor_tensor(out=Qt[64:128, 2 * hp_ + 1, :], in0=r1[64:128, i, :N], in1=r2[64:128, i, :N], op=ALU.add), reads=[r1_lb[i], r2_lb[i]], writes=dst_lbs)
                elif out32 is None:
                    P.op("dve", lambda h: h.tensor_tensor(out=dst_bf, in0=a1, in1=a2, op=ALU.add), reads=[r1_lb[i], r2_lb[i]], writes=dst_lbs)
                else:
                    dap, p0, p1, c0, c1 = out32
                    P.op("dve", lambda h: h.tensor_tensor(out=r1[:, i, :N], in0=r1[:, i, :N], in1=r2[:, i, :N], op=ALU.add), reads=[r1_lb[i], r2_lb[i]], writes=[r1_lb[i]])
                    P.op("act", lambda h: h.copy(out=dst_bf, in_=a1), reads=[r1_lb[i]], writes=dst_lbs)
                    P.dma("sp", dap, r1[p0:p1, i, c0:c1], reads=[r1_lb[i]], sem=f"ko{i}")

            sub = cfg.get("attn_sub", 9)
            s0, s0_lb = ws.get(tiles[0])
            make_rot(wring[(ws.base + tiles[0]) % NSLOT][:, 0:4096], wrot[:, 0:4096], s0_lb)
            rv = wrot[:, 0:4096].rearrange("p (a b) -> p a b", a=KT)
            for hp in range(4):
                proj_rope(s0, s0_lb, rv, wrot_lb, hp * 128, QA[:, hp, :], [qa_lb], qz=(QA, hp))
            if sub < 2:
                layer_norm(1, N); P.barrier(); return
            s1, s1_lb = ws.get(tiles[1])
            make_rot(wring[(ws.base + tiles[1]) % NSLOT][:, 0:4096], wrot[:, 0:4096], s1_lb)
            for hp in range(4):
                if is_sample:
                    o32 = (sakT[hp * 128:(hp + 1) * 128, :], 0, 128, 0, N)
                    proj_rope(s1, s1_lb, rv, wrot_lb, hp * 128, KTn[:, hp, :], [ktn_lb], out32=o32)
                else:
                    o32 = (pakT[hp * 128:(hp + 1) * 128, t0 - 2048:t0 - 2048 + N], 0, 128, 0, N) if (t0 >= 2048 and not cfg.get("skipkout")) else None
                    rb0 = gb0 % NRB
                    proj_rope(s1, s1_lb, rv, wrot_lb, hp * 128, KTs[:, hp, rb0 * 128:rb0 * 128 + N], kt_lb[rb0:rb0 + nb], out32=o32)
            if sub < 3:
                layer_norm(1, N); P.barrier(); return
            s2, s2_lb = ws.get(tiles[2])
            for tb in range(nb):
                i = cnt[0] % 2
                cnt[0] += 1
                pv = ps[:, i, :]
                for k in range(KT):
                    P.op("pe", lambda h, k=k: h.matmul(pv, lhsT=xb[:, k, tb * 128:(tb + 1) * 128], rhs=s2[:, k, :], start=(k == 0), stop=(k == KT - 1)),
                         reads=[s2_lb, xb_lb[k]], writes=[ps_lb[i]], inc=(k == KT - 1))
                if is_sample:
                    P.op("act", lambda h: h.copy(out=Vn[:, 0:512], in_=pv), reads=[ps_lb[i]], writes=[ktn_lb])
                    P.op("act", lambda h: h.activation(out=vstg[:, i, :], in_=pv, func=AF.Copy), reads=[ps_lb[i]], writes=[vstg_lb[i]])
                    P.dma("sp", sav[:, :], vstg[:, i, :], reads=[vstg_lb[i]], sem=f"vo{i}")
                else:
                    gb = gb0 + tb
                    P.op("act", lambda h, gb=gb: h.copy(out=Vst[:, gb % NRB, :], in_=pv), reads=[ps_lb[i]], writes=[v_lb[gb % NRB]])
                    if t0 >= 2048 and not cfg.get("skipvout"):
                        P.op("act", lambda h: h.activation(out=vstg[:, i, :], in_=pv, func=AF.Copy), reads=[ps_lb[i]], writes=[vstg_lb[i]])
                        r0 = t0 - 2048 + tb * 128
                        P.dma("sp", pav[r0:r0 + 128, :], vstg[:, i, :], reads=[vstg_lb[i]], sem=f"vo{i}")
            if sub < 4:
                layer_norm(1, N); P.barrier(); return
            s3, s3_lb = ws.get(tiles[3])
            make_rot(wring[(ws.base + tiles[3]) % NSLOT][:, 0:4096], wrot[:, 0:4096], s3_lb)
            for hp in range(4):
                proj_rope(s3, s3_lb, rv, wrot_lb, hp * 128, QB[:, hp, :], [qb_lb], qz=(QB, hp))
            if sub < 5:
                layer_norm(1, N); P.barrier(); return
            s4, s4_lb = ws.get(tiles[4])
            for kv in range(2):
                for dp in range(2):
                    P.op("dve", lambda h, kv=kv, dp=dp: h.tensor_copy(out=wdup[:, :, kv, dp * 64:(dp + 1) * 64], in_=s4[:, :, kv * 64:(kv + 1) * 64]),
                         reads=[s4_lb], writes=[wdup_lb])
            dsv = wrot[:, 0:2048].rearrange("p (a t i) -> p a t i", t=4, i=32)
            drv = wrot[:, 2048:4096].rearrange("p (a t i) -> p a t i", t=4, i=32)
            for dp in range(2):
                P.op("act", lambda h, dp=dp: h.mul(drv[:, :, 2 * dp, :], dsv[:, :, 2 * dp + 1, :], -1.0), reads=[wdup_lb], writes=[wdup_lb])
                P.op("dve", lambda h, dp=dp: h.tensor_copy(out=drv[:, :, 2 * dp + 1, :], in_=dsv[:, :, 2 * dp, :]), reads=[wdup_lb], writes=[wdup_lb])
            for kv in range(2):
                wz = wdup[:, :, kv, :]
                wr = wdupr[:, :, kv, :]
                if is_sample:
                    proj_rope(wz, wdup_lb, wr, wdup_lb, 0, KTBn[:, kv, :], [ktn_lb], out32=(sbkT[kv * 64:(kv + 1) * 64, :], 0, 64, 0, N))
                else:
                    o32 = (pbkT[kv * 64:(kv + 1) * 64, :], 0, 64, N - 128, N) if gb0 + nb == 32 else None
                    proj_rope(wz, wdup_lb, wr, wdup_lb, 0, KTB[:, kv, (gb0 % 8):(gb0 % 8) + nb, :], ktb_lb[gb0 % 8:(gb0 % 8) + nb], out32=o32)
            for tb in range(nb):
                i = cnt[0] % 2
                cnt[0] += 1
                pv = ps[:, i, 0:128]
                for k in range(KT):
                    P.op("pe", lambda h, k=k: h.matmul(pv, lhsT=xb[:, k, tb * 128:(tb + 1) * 128], rhs=s4[:, k, 128:256], start=(k == 0), stop=(k == KT - 1)),
                         reads=[s4_lb, xb_lb[k]], writes=[ps_lb[i]], inc=(k == KT - 1))
                if is_sample:
                    P.op("act", lambda h: h.copy(out=Vn[:, 512:640], in_=pv), reads=[ps_lb[i]], writes=[ktn_lb])
                    P.op("act", lambda h: h.activation(out=vstg[:, i, 0:128], in_=pv, func=AF.Copy), reads=[ps_lb[i]], writes=[vstg_lb[i]])
                    P.dma("sp", sbv[:, :], vstg[:, i, 0:128], reads=[vstg_lb[i]], sem=f"vo{i}")
                else:
                    gb = gb0 + tb
                    P.op("act", lambda h, gb=gb: h.copy(out=VB[:, gb % 8, :], in_=pv), reads=[ps_lb[i]], writes=[vb_lb[gb % 8]])
                    if gb == 31:
                        P.op("act", lambda h: h.activation(out=vstg[:, i, 0:128], in_=pv, func=AF.Copy), reads=[ps_lb[i]], writes=[vstg_lb[i]])
                        P.dma("sp", pbv[:, :], vstg[:, i, 0:128], reads=[vstg_lb[i]], sem=f"vo{i}")
            def attend(qtile, q_lb, qrow0, qhp, qcols, nq, nheads, key_blocks, obank, dbank, sink_cols):
                W = nheads * nq
                P.op("pe", lambda h: h.matmul(ps[0:64, obank, 0:512], lhsT=zrow[0:1, 0:64], rhs=zrow[0:1, 0:512], start=True, stop=False, skip_group_check=True),
                     reads=[const_lb], writes=[ps_lb[obank]], inc=False)
                P.op("pe", lambda h: h.matmul(ps[0:64, dbank, 0:512], lhsT=zrow[0:1, 0:64], rhs=zrow[0:1, 0:512], start=True, stop=False, skip_group_check=True),
                     reads=[const_lb], writes=[ps_lb[dbank]], inc=True)
                for bi, (kfn, vfn, mask_ap, nk) in enumerate(key_blocks):
                    sbk = 4 + (cnt[0] % 2)
                    pi = cnt[0] % 2
                    cnt[0] += 1
                    for hh in range(nheads):
                        kap, klbs = kfn(hh)
                        P.op("pe", lambda h, kap=kap, hh=hh: h.matmul(ps[0:nk, sbk, hh * nq:(hh + 1) * nq], lhsT=kap,
                                                                      rhs=qtile[:, qhp(hh), qcols[0]:qcols[1]], start=True, stop=True),
                             reads=klbs + [q_lb], writes=[ps_lb[sbk]], inc=(hh == nheads - 1))
                    P.op("act", lambda h: h.activation(out=PT[0:nk, pi, 0:W], in_=ps[0:nk, sbk, 0:W], func=AF.Exp, scale=SC),
                         reads=[ps_lb[sbk]], writes=[pt_lb[pi]])
                    pv3 = PT[0:nk, pi, 0:W].rearrange("p (a b) -> p a b", a=nheads)
                    P.op("dve", lambda h, pv3=pv3, mask_ap=mask_ap: h.tensor_tensor(out=pv3, in0=pv3, in1=mask_ap.unsqueeze(1).to_broadcast([nk, nheads, nq]), op=ALU.mult),
                         reads=[pt_lb[pi], const_lb], writes=[pt_lb[pi]])
                    last = bi == len(key_blocks) - 1
                    for hh in range(nheads):
                        vap, vlbs = vfn(hh)
                        P.op("pe", lambda h, vap=vap, hh=hh: h.matmul(ps[0:64, obank, hh * nq:(hh + 1) * nq], lhsT=vap, rhs=PT[0:nk, pi, hh * nq:(hh + 1) * nq],
                                                                      start=False, stop=last, skip_group_check=True),
                             reads=vlbs + [pt_lb[pi]], writes=[ps_lb[obank]], inc=False)
                    P.op("pe", lambda h: h.matmul(ps[0:64, dbank, 0:W], lhsT=ones_bf[0:nk, 0:64], rhs=PT[0:nk, pi, 0:W], start=False, stop=last, skip_group_check=True),
                         reads=[pt_lb[pi], const_lb], writes=[ps_lb[dbank]], inc=True)
                if sink_cols is not None:
                    for hh in range(nheads):
                        c = sink_cols[0] + hh
                        P.op("dve", lambda h, hh=hh, c=c: h.tensor_scalar(out=rd[:, hh * nq:(hh + 1) * nq], in0=ps[0:64, dbank, hh * nq:(hh + 1) * nq],
                                                                       scalar1=sink_e[0:64, c:c + 1], scalar2=None, op0=ALU.add),
                             reads=[ps_lb[dbank], const_lb], writes=[rd_lb])
                    P.op("dve", lambda h: h.reciprocal(out=rd[:, 0:W], in_=rd[:, 0:W]), reads=[rd_lb], writes=[rd_lb])
                else:
                    P.op("dve", lambda h: h.reciprocal(out=rd[:, 0:W], in_=ps[0:64, dbank, 0:W]), reads=[ps_lb[dbank]], writes=[rd_lb])

            def finalize(head0, nheads, nq, qc):
                for hh in range(nheads):
                    P.op("dve", lambda h, hh=hh: h.tensor_tensor(out=oT[:, head0 + hh, qc[0]:qc[1]], in0=ps[0:64, 6, hh * nq:(hh + 1) * nq],
                                                                 in1=rd[:, hh * nq:(hh + 1) * nq], op=ALU.mult),
                         reads=[ps_lb[6], rd_lb], writes=[ot_lb])

            lvl = cfg.get("attn_level", 9)
            if lvl < 3:
                P.op("dve", lambda h: h.memset(oT[:], 0.0), writes=[ot_lb])
            if not is_sample:
                for qb in range(nb if lvl >= 1 else 0):
                    gq = gb0 + qb
                    qc = (qb * 128, (qb + 1) * 128)
                    for hg in range(2):
                        kbl = []
                        for kb in range(max(0, gq - 16), gq + 1):
                            dl = gq - kb
                            rk = kb % NRB
                            kbl.append((lambda hh, rk=rk, hg=hg: (KTs[:, (4 * hg + hh) // 2, rk * 128:(rk + 1) * 128], [kt_lb[rk]]),
                                        lambda hh, rk=rk, hg=hg: (Vst[:, rk, (4 * hg + hh) * 64:(4 * hg + hh + 1) * 64], [v_lb[rk]]),
                                        mA[:, dl, :], 128))
                        ob, db = 6, 7
                        attend(QA, qa_lb, None, lambda hh, hg=hg: 4 * hg + hh, qc, 128, 4, kbl, ob, db, None)
                        finalize(4 * hg, 4, 128, qc)
                    for kv in range(2 if lvl >= 2 else 0):
                        kbl = []
                        for kb in range(max(0, gq - 1), gq + 1):
                            dl = gq - kb
                            kbl.append((lambda hh, kb=kb, kv=kv: (KTB[:, kv, kb % 8, :], [ktb_lb[kb % 8]]),
                                        lambda hh, kb=kb, kv=kv: (VB[:, kb % 8, kv * 64:(kv + 1) * 64], [vb_lb[kb % 8]]),
                                        mB[:, dl, :], 128))
                        attend(QB, qb_lb, None, lambda hh, kv=kv: 4 * kv + hh, qc, 128, 4, kbl, 6, 7, (4 * kv, 4 * kv + 4))
                        finalize(8 + 4 * kv, 4, 128, qc)
            else:
                def load_seq(sq):
                    par = sq % 2
                    for hf in range(2):
                        P.dma("pool", KTs[:, :, hf * 1024:(hf + 1) * 1024], cakT[sq].rearrange("(hp p) t -> p hp t", p=128)[:, :, hf * 1024:(hf + 1) * 1024],
                              writes=[kt_lb[hf]], sem=f"ck{hf}")
                        P.dma("pool", Vst[:, hf * 8:(hf + 1) * 8, :], cav[sq].rearrange("(b p) n -> p b n", p=128)[:, hf * 8:(hf + 1) * 8, :],
                              writes=[v_lb[hf]], sem=f"cv{hf}")
                    P.dma("pool", KBs[0:64, par, :, :], cbkT[sq].rearrange("(kv d) t -> d kv t", d=64), writes=[kbs_lb[par]], sem=f"cb{par}")
                    P.dma("pool", KBs[64:128, par, :, :], cbkT[sq].rearrange("(kv d) t -> d kv t", d=64), writes=[kbs_lb[par]], sem=f"cb{par}")
                    P.dma("pool", VBs[:, par, :], cbv[sq], writes=[kbs_lb[par]], sem=f"cb{par}")
                    P.dma("sp", Vsq[:, par, :], Vn[sq * 8:(sq + 1) * 8, :], reads=[ktn_lb], writes=[vsq_lb[par]], sem=f"vq{par}")

                load_seq(0)
                for sq in range(16):
                    par = sq % 2
                    qc = (sq * 8, sq * 8 + 8)
                    kbl = []
                    for kb in range(16):
                        kbl.append((lambda hh, kb=kb: (KTs[:, hh // 2, kb * 128:(kb + 1) * 128], [kt_lb[kb // 8]]),
                                    lambda hh, kb=kb: (Vst[:, kb, hh * 64:(hh + 1) * 64], [v_lb[kb // 8]]),
                                    msA[:, kb, :], 128))
                    kbl.append((lambda hh, sq=sq: (KTn[:, hh // 2, sq * 8:sq * 8 + 8], [ktn_lb]),
                                lambda hh, par=par: (Vsq[0:8, par, hh * 64:(hh + 1) * 64], [vsq_lb[par]]),
                                msA[0:8, 16, :], 8))
                    attend(QA, qa_lb, None, lambda hh: hh, qc, 8, 8, kbl, 6, 7, None)
                    if sq + 1 < 16:
                        load_seq(sq + 1)
                    finalize(0, 8, 8, qc)
                    for kv in range(2):
                        kbl = [(lambda hh, kv=kv, par=par: (KBs[:, par, kv, :], [kbs_lb[par]]),
                                lambda hh, kv=kv, par=par: (VBs[:, par, kv * 64:(kv + 1) * 64], [kbs_lb[par]]),
                                msB[:, 0, :], 128),
                               (lambda hh, kv=kv, sq=sq: (KTBn[:, kv, sq * 8:sq * 8 + 8], [ktn_lb]),
                                lambda hh, kv=kv, par=par: (Vsq[0:8, par, 512 + kv * 64:512 + (kv + 1) * 64], [vsq_lb[par]]),
                                msB[0:8, 1, :], 8)]
                        attend(QB, qb_lb, None, lambda hh, kv=kv: 4 * kv + hh, qc, 8, 4, kbl, 6, 7, (4 * kv, 4 * kv + 4))
                        finalize(8 + 4 * kv, 4, 8, qc)

            if sub < 6:
                layer_norm(1, N); P.barrier(); return
            for o in range(KT):
                so, so_lb = ws.get(wo_t[o])
                so = wring[(ws.base + wo_t[o]) % NSLOT][0:64, 0:2048].rearrange("p (a b) -> p a b", a=16)
                b = o % 2
                po = ps[:, b, :N]
                if cfg.get("wo_mode", 9) < 1:
                    continue
                for hh in range(16):
                    P.op("pe", lambda h, hh=hh, so=so: h.matmul(po, lhsT=so[:, hh, :], rhs=oT[:, hh, :], start=(hh == 0), stop=(hh == 15)),
                         reads=[so_lb, ot_lb], writes=[ps_lb[b]], inc=(hh == 15))
                if cfg.get("wo_mode", 9) < 2:
                    continue
                P.op("dve", lambda h, o=o, po=po: h.scalar_tensor_tensor(out=x32[:, o, :N], in0=po, scalar=1.0, in1=x32[:, o, :N],
                                                                         op0=ALU.mult, op1=ALU.add),
                     reads=[ps_lb[b], x32_lb[o]], writes=[x32_lb[o]])
            layer_norm(1, N)
            P.barrier()


    TWO_PI = 2.0 * math.pi
    PI_S = math.pi * (1.0 - 1e-6)
    if stage >= 3:
        s_win = din("s_win", [D, D])
        s_wglu = din("s_wglu", [D, D])
        s_wout = din("s_wout", [D, D])
        s_lre = din("s_lre", [128, 32])
        s_lim = din("s_lim", [128, 32])
        s_ldt = din("s_ldt", [128, 32])
        s_Bre = din("s_Bre", [128, 4096])
        s_Bim = din("s_Bim", [128, 4096])
        s_Cre = din("s_Cre", [128, 4096])
        s_Cim = din("s_Cim", [128, 4096])
        s_d = din("s_d", [128, 8])
        s_bglu = din("s_bglu", [128, 8])
        tau1_d = din("tau1", [128, 512])
        smask_d = din("smask", [128, 128])
        st_re_d = din("st_re", [128, 512])
        st_im_d = din("st_im", [128, 512])
        tw_cr = nc.dram_tensor("tw_cr", [32, 128, 512], F32, kind="Internal").ap()
        tw_si = nc.dram_tensor("tw_si", [32, 128, 512], F32, kind="Internal").ap()
        pc_o = dout("pc", [128, 64])
        sc_o = dout("sc", [128, 1024])
        Bre_sb = sb("Bre_sb", [128, 32, 128], BF16)
        Bim_sb = sb("Bim_sb", [128, 32, 128], BF16)
        Cpr = sb("Cpr", [128, 32, 128], BF16)
        Cni = sb("Cni", [128, 32, 128], BF16)
        rdec = sb("rdec", [128, 32], F32)
        f_re = sb("f_re", [128, 32], F32)
        f_im = sb("f_im", [128, 32], F32)
        car = sb("car", [128, 2, 32], F32)
        crs = sb("crs", [128, 32, 8], F32)
        sis = sb("sis", [128, 32, 8], F32)
        h0t = sb("h0t", [128, 2, 32, 16], F32)
        d_sb = sb("d_sb", [128, 8], F32)
        bglu_sb = sb("bglu_sb", [128, 8], F32)
        smask = sb("smask", [128, 128], F32)
        npi = sb("npi", [128, 1], F32)
        S = LB()
        car_lb = LB()

        def sop(e, fn):
            P.op(e, fn, reads=[S], writes=[S])

        with contextlib.ExitStack() as fs:
            lre = sb("lre", [128, 32], F32, fs)
            lim = sb("lim", [128, 32], F32, fs)
            dtt = sb("dtt", [128, 32], F32, fs)
            thn = sb("thn", [128, 32], F32, fs)
            cs = sb("cs", [128, 32], F32, fs)
            sn = sb("sn", [128, 32], F32, fs)
            a1 = sb("a1", [128, 32], F32, fs)
            a2 = sb("a2", [128, 32], F32, fs)
            a3 = sb("a3", [128, 32], F32, fs)
            a4 = sb("a4", [128, 32], F32, fs)
            tau1 = sb("tau1", [128, 512], F32, fs)
            gu = sb("gu", [128, 512], F32, fs)
            gn = sb("gn", [128, 512], F32, fs)
            gf = sb("gf", [128, 512], F32, fs)
            gm = sb("gm", [128, 512], F32, fs)
            gtab = sb("gtab", [128, 2, 2, 512], F32, fs)
            gtab_lb = [lbs(2), lbs(2)]
            for t_, d_ in ((lre, s_lre), (lim, s_lim), (dtt, s_ldt), (d_sb, s_d), (bglu_sb, s_bglu), (tau1, tau1_d), (smask, smask_d)):
                P.dma("sp", t_[:], d_[:, :], writes=[S], sem="sc")
            P.dma("sp", h0t[:, 0].rearrange("p a b -> p (a b)"), st_re_d[:, :], writes=[S], sem="sc")
            P.dma("sp", h0t[:, 1].rearrange("p a b -> p (a b)"), st_im_d[:, :], writes=[S], sem="sc")
            P.dma("pool", Bre_sb[:].rearrange("p a b -> p (a b)"), s_Bre[:, :], writes=[S], sem="sc2")
            P.dma("pool", Bim_sb[:].rearrange("p a b -> p (a b)"), s_Bim[:, :], writes=[S], sem="sc2")
            sop("dve", lambda h: h.memset(npi[:], -PI_S))
            sop("dve", lambda h: h.memset(car[:], 0.0))
            sop("act", lambda h: h.activation(out=dtt[:], in_=dtt[:], func=AF.Exp))
            sop("dve", lambda h: h.tensor_tensor(out=a1[:], in0=lre[:], in1=dtt[:], op=ALU.mult))
            sop("act", lambda h: h.activation(out=rdec[:], in_=a1[:], func=AF.Exp))
            sop("dve", lambda h: h.tensor_tensor(out=a1[:], in0=lim[:], in1=dtt[:], op=ALU.mult))
            sop("dve", lambda h: h.tensor_scalar(out=thn[:], in0=a1[:], scalar1=1.0 / TWO_PI, scalar2=None, op0=ALU.mult))

            def gen_table(i, shift, dst, dst_lb):
                P.op("dve", lambda h: h.tensor_scalar(out=gu[:], in0=tau1[:], scalar1=thn[:, i:i + 1], scalar2=shift, op0=ALU.mult, op1=ALU.add), reads=[S], writes=[S])
                sop("dve", lambda h: h.tensor_scalar(out=gn[:], in0=gu[:], scalar1=8388608.0, scalar2=None, op0=ALU.add))
                sop("dve", lambda h: h.tensor_scalar(out=gm[:], in0=gn[:], scalar1=8388608.0, scalar2=None, op0=ALU.subtract))
                sop("dve", lambda h: h.tensor_tensor(out=gf[:], in0=gu[:], in1=gm[:], op=ALU.subtract))
                sop("dve", lambda h: h.tensor_scalar(out=gn[:], in0=gf[:], scalar1=0.0, scalar2=None, op0=ALU.is_lt))
                sop("dve", lambda h: h.tensor_tensor(out=gu[:], in0=gf[:], in1=gn[:], op=ALU.add))
                P.op("act", lambda h: h.activation(out=dst, in_=gu[:], func=AF.Sin, scale=TWO_PI * (1.0 - 1e-6), bias=npi[:, 0:1]),
                     reads=[S], writes=[S, dst_lb])

            for i in range(32):
                b = i % 2
                gen_table(i, 0.5, gtab[:, b, 1, :], gtab_lb[b][1])
                gen_table(i, 0.75, gtab[:, b, 0, :], gtab_lb[b][0])
                P.dma("sp", tw_si[i], gtab[:, b, 1, :], reads=[gtab_lb[b][1]], sem=f"tws{b}")
                P.dma("sp", tw_cr[i], gtab[:, b, 0, :], reads=[gtab_lb[b][0]], sem=f"twc{b}")
                P.op("act", lambda h, i=i, b=b: h.copy(out=crs[:, i, :], in_=gtab[:, b, 0, 0:8]), reads=[gtab_lb[b][0]], writes=[S])
                P.op("act", lambda h, i=i, b=b: h.copy(out=sis[:, i, :], in_=gtab[:, b, 1, 0:8]), reads=[gtab_lb[b][1]], writes=[S])
            sop("dve", lambda h: h.tensor_copy(out=cs[:], in_=crs[:, :, 0]))
            sop("dve", lambda h: h.tensor_copy(out=sn[:], in_=sis[:, :, 0]))
            sop("dve", lambda h: h.tensor_tensor(out=a1[:], in0=rdec[:], in1=cs[:], op=ALU.mult))
            sop("dve", lambda h: h.tensor_scalar(out=a1[:], in0=a1[:], scalar1=-1.0, scalar2=None, op0=ALU.add))
            sop("dve", lambda h: h.tensor_tensor(out=a2[:], in0=rdec[:], in1=sn[:], op=ALU.mult))
            sop("dve", lambda h: h.tensor_tensor(out=a3[:], in0=lre[:], in1=lre[:], op=ALU.mult))
            sop("dve", lambda h: h.tensor_tensor(out=a4[:], in0=lim[:], in1=lim[:], op=ALU.mult))
            sop("dve", lambda h: h.tensor_tensor(out=a3[:], in0=a3[:], in1=a4[:], op=ALU.add))
            sop("dve", lambda h: h.reciprocal(out=a3[:], in_=a3[:]))
            sop("dve", lambda h: h.tensor_tensor(out=f_re[:], in0=a1[:], in1=lre[:], op=ALU.mult))
            sop("dve", lambda h: h.tensor_tensor(out=a4[:], in0=a2[:], in1=lim[:], op=ALU.mult))
            sop("dve", lambda h: h.tensor_tensor(out=f_re[:], in0=f_re[:], in1=a4[:], op=ALU.add))
            sop("dve", lambda h: h.tensor_tensor(out=f_re[:], in0=f_re[:], in1=a3[:], op=ALU.mult))
            sop("dve", lambda h: h.tensor_tensor(out=f_im[:], in0=a2[:], in1=lre[:], op=ALU.mult))
            sop("dve", lambda h: h.tensor_tensor(out=a4[:], in0=a1[:], in1=lim[:], op=ALU.mult))
            sop("dve", lambda h: h.tensor_tensor(out=f_im[:], in0=f_im[:], in1=a4[:], op=ALU.subtract))
            sop("dve", lambda h: h.tensor_tensor(out=f_im[:], in0=f_im[:], in1=a3[:], op=ALU.mult))
            c32 = sb("c32", [128, 2, 8, 128], F32, fs)
            ct = sb("ct", [128, 2, 8, 128], F32, fs)
            for ch in range(4):
                i0 = ch * 8
                P.dma("sp", c32[:, 0].rearrange("p a b -> p (a b)"), s_Cre[:, i0 * 128:(i0 + 8) * 128], reads=[S], writes=[S], sem="sc")
                P.dma("sp", c32[:, 1].rearrange("p a b -> p (a b)"), s_Cim[:, i0 * 128:(i0 + 8) * 128], reads=[S], writes=[S], sem="sc")
                frb = f_re[:, i0:i0 + 8].unsqueeze(2).to_broadcast([128, 8, 128])
                fib = f_im[:, i0:i0 + 8].unsqueeze(2).to_broadcast([128, 8, 128])
                sop("dve", lambda h, frb=frb: h.tensor_tensor(out=ct[:, 0], in0=c32[:, 0], in1=frb, op=ALU.mult))
                sop("dve", lambda h, fib=fib: h.tensor_tensor(out=ct[:, 1], in0=c32[:, 1], in1=fib, op=ALU.mult))
                sop("dve", lambda h, i0=i0: h.tensor_tensor(out=Cpr[:, i0:i0 + 8, :], in0=ct[:, 0], in1=ct[:, 1], op=ALU.subtract))
                sop("dve", lambda h, fib=fib: h.tensor_tensor(out=ct[:, 0], in0=c32[:, 0], in1=fib, op=ALU.mult))
                sop("dve", lambda h, frb=frb: h.tensor_tensor(out=ct[:, 1], in0=c32[:, 1], in1=frb, op=ALU.mult))
                sop("dve", lambda h, i0=i0: h.scalar_tensor_tensor(out=Cni[:, i0:i0 + 8, :], in0=ct[:, 0], scalar=-1.0, in1=ct[:, 1], op0=ALU.mult, op1=ALU.subtract))
            sop("dve", lambda h: h.tensor_tensor(out=a1[:], in0=f_re[:], in1=f_re[:], op=ALU.mult))
            sop("dve", lambda h: h.tensor_tensor(out=a2[:], in0=f_im[:], in1=f_im[:], op=ALU.mult))
            sop("dve", lambda h: h.tensor_tensor(out=a1[:], in0=a1[:], in1=a2[:], op=ALU.add))
            sop("dve", lambda h: h.reciprocal(out=a1[:], in_=a1[:]))
            sop("dve", lambda h: h.tensor_tensor(out=a2[:], in0=f_re[:], in1=a1[:], op=ALU.mult))
            sop("dve", lambda h: h.scalar_tensor_tensor(out=a3[:], in0=f_im[:], scalar=-1.0, in1=a1[:], op0=ALU.mult, op1=ALU.mult))
            grb = a2[:, :].unsqueeze(2).to_broadcast([128, 32, 16])
            gib = a3[:, :].unsqueeze(2).to_broadcast([128, 32, 16])
            hh0 = sb("hh0", [128, 4, 32, 16], F32, fs)
            sop("dve", lambda h: h.tensor_tensor(out=hh0[:, 0], in0=h0t[:, 0], in1=grb, op=ALU.mult))
            sop("dve", lambda h: h.tensor_tensor(out=hh0[:, 1], in0=h0t[:, 1], in1=gib, op=ALU.mult))
            sop("dve", lambda h: h.tensor_tensor(out=hh0[:, 2], in0=h0t[:, 0], in1=gib, op=ALU.mult))
            sop("dve", lambda h: h.tensor_tensor(out=hh0[:, 3], in0=h0t[:, 1], in1=grb, op=ALU.mult))
            sop("dve", lambda h: h.tensor_tensor(out=h0t[:, 0], in0=hh0[:, 0], in1=hh0[:, 1], op=ALU.subtract))
            sop("dve", lambda h: h.tensor_tensor(out=h0t[:, 1], in0=hh0[:, 2], in1=hh0[:, 3], op=ALU.add))
            P.barrier()

    def ssm(N, is_sample):
        with contextlib.ExitStack() as fs:
            uT = sb("uT", [128, KT, N], BF16, fs)
            uT_lb = lbs(KT)
            tw = sb("tw", [128, 1, 2, N], F32, fs)
            tw_lb = lbs(2)
            tt = sb("tt", [128, 4, N], F32, fs)
            tt_lb = lbs(4)
            bu = sb("bu", [128, 2, N], F32, fs)
            bu_lb = lbs(2)
            LL = sb("LL", [128, 2, N], F32, fs)
            ll_lb = lbs(2)
            hb = sb("hb", [128, 2, 2, N], BF16, fs)
            hb_lb = lbs(2)
            yb = sb("yb", [128, KT, N], BF16, fs)
            yb_lb = lbs(KT)
            gts = [ttmp[:, 0, :N], ttmp[:, 1, :N], ystage[:, 0, :N]]
            gt_lb = [ttmp_lb[0], ttmp_lb[1], ystage_lb[0]]
            if is_sample:
                rz = sb("rz", [128, 128], F32, fs)
                rz_lb = LB()
                psb = sb("psb", [128, 2, 128], F32, fs)
                psb_lb = lbs(2)
            ws = new_stream()
            mk = lambda dram: [ws.add(dram.rearrange("(k p) n -> p k n", p=128)[:, :, i * 512:(i + 1) * 512], (KT, 512)) for i in range(2)]
            t_in, t_glu, t_out = mk(s_win), mk(s_wglu), mk(s_wout)
            start_stream(ws)
            cnt = [0]

            def linear(tiles, src, src_lb, evac):
                for o in range(KT):
                    wsl, wsl_lb = ws.get(tiles[o // 4])
                    b = cnt[0] % 2
                    cnt[0] += 1
                    po = ps[:, b, :N]
                    for k in range(KT):
                        P.op("pe", lambda h, k=k, wsl=wsl, po=po, o=o: h.matmul(po, lhsT=wsl[:, k, (o % 4) * 128:(o % 4 + 1) * 128], rhs=src[:, k, :N],
                                                                           start=(k == 0), stop=(k == KT - 1)),
                             reads=[wsl_lb, src_lb[k]], writes=[ps_lb[b]], inc=(k == KT - 1))
                    evac(o, po, ps_lb[b])

            linear(t_in, xb, xb_lb, lambda o, po, plb: P.op("act", lambda h: h.copy(out=uT[:, o, :], in_=po), reads=[plb], writes=[uT_lb[o]]))

            for i in range(32):
                j, q = i // 4, i % 4
                hbase, w = 64 * (q // 2), q % 2
                tb = 0
                if not is_sample:
                    P.dma("sp", tw[:, tb, 0, :], tw_cr[i], writes=[tw_lb[tb]], sem=f"tl{tb}")
                    P.dma("sp", tw[:, tb, 1, :], tw_si[i], writes=[tw_lb[tb]], sem=f"tl{tb}")
                    tc, ts_ = tw[:, tb, 0, :], tw[:, tb, 1, :]
                    tlb = [tw_lb[tb]]
                    v2 = lambda ap: ap
                else:
                    tc = crs[:, i, :].unsqueeze(1).to_broadcast([128, 16, 8])
                    ts_ = sis[:, i, :].unsqueeze(1).to_broadcast([128, 16, 8])
                    tlb = [S]
                    v2 = lambda ap: ap.rearrange("p (a b) -> p a b", b=8)
                pre, pim = ps[:, 2, :N], ps[:, 3, :N]
                P.op("pe", lambda h: h.matmul(pre, lhsT=Bre_sb[:, i, :], rhs=uT[:, j, :], start=True, stop=True),
                     reads=[S, uT_lb[j]], writes=[ps_lb[2]], inc=True)
                P.op("pe", lambda h: h.matmul(pim, lhsT=Bim_sb[:, i, :], rhs=uT[:, j, :], start=True, stop=True),
                     reads=[S, uT_lb[j]], writes=[ps_lb[3]], inc=True)
                if is_sample:
                    P.op("act", lambda h: h.copy(out=psb[:, 0, :], in_=pre), reads=[ps_lb[2]], writes=[psb_lb[0]])
                    P.op("act", lambda h: h.copy(out=psb[:, 1, :], in_=pim), reads=[ps_lb[3]], writes=[psb_lb[1]])
                    sre, sim_, sre_lb, sim_lb = v2(psb[:, 0, :]), v2(psb[:, 1, :]), psb_lb[0], psb_lb[1]
                else:
                    sre, sim_, sre_lb, sim_lb = pre, pim, ps_lb[2], ps_lb[3]
                P.op("dve", lambda h: h.tensor_tensor(out=v2(tt[:, 0, :]), in0=sre, in1=tc, op=ALU.mult), reads=[sre_lb] + tlb, writes=[tt_lb[0]])
                P.op("dve", lambda h: h.tensor_tensor(out=v2(tt[:, 1, :]), in0=sim_, in1=ts_, op=ALU.mult), reads=[sim_lb] + tlb, writes=[tt_lb[1]])
                P.op("dve", lambda h: h.tensor_tensor(out=v2(tt[:, 2, :]), in0=sim_, in1=tc, op=ALU.mult), reads=[sim_lb] + tlb, writes=[tt_lb[2]])
                P.op("dve", lambda h: h.tensor_tensor(out=v2(tt[:, 3, :]), in0=sre, in1=ts_, op=ALU.mult), reads=[sre_lb] + tlb, writes=[tt_lb[3]])
                P.op("dve", lambda h: h.tensor_tensor(out=bu[:, 0, :], in0=tt[:, 0, :], in1=tt[:, 1, :], op=ALU.add), reads=[tt_lb[0], tt_lb[1]], writes=[bu_lb[0]])
                P.op("dve", lambda h: h.tensor_tensor(out=bu[:, 1, :], in0=tt[:, 2, :], in1=tt[:, 3, :], op=ALU.subtract), reads=[tt_lb[2], tt_lb[3]], writes=[bu_lb[1]])
                if is_sample:
                    P.op("dve", lambda h: h.tensor_scalar(out=rz[:], in0=smask[:], scalar1=rdec[:, i:i + 1], scalar2=None, op0=ALU.mult), reads=[S], writes=[rz_lb])
                    for c in range(2):
                        b0 = v2(bu[:, c, :])[:, :, 0]
                        P.op("dve", lambda h, c=c, b0=b0: h.scalar_tensor_tensor(out=v2(tt[:, c, :])[:, :, 0], in0=h0t[:, c, i, :], scalar=rdec[:, i:i + 1], in1=b0,
                                                                          op0=ALU.mult, op1=ALU.add), reads=[S, bu_lb[c]], writes=[tt_lb[c]])
                        P.op("dve", lambda h, c=c, b0=b0: h.tensor_copy(out=b0, in_=v2(tt[:, c, :])[:, :, 0]), reads=[tt_lb[c]], writes=[bu_lb[c]])
                    for c in range(2):
                        P.op("dve", lambda h, c=c: h.tensor_tensor_scan(out=LL[:, c, :], data0=rz[:], data1=bu[:, c, :], initial=0.0, op0=ALU.mult, op1=ALU.add),
                             reads=[rz_lb, bu_lb[c]], writes=[ll_lb[c]])
                else:
                    for c in range(2):
                        P.op("dve", lambda h, c=c: h.tensor_tensor_scan(out=LL[:, c, :], data0=rdec[:, i:i + 1].to_broadcast([128, N]), data1=bu[:, c, :],
                                                                        initial=car[:, c, i:i + 1], op0=ALU.mult, op1=ALU.add),
                             reads=[S, car_lb, bu_lb[c]], writes=[ll_lb[c]])
                Lr, Li = v2(LL[:, 0, :]), v2(LL[:, 1, :])
                P.op("dve", lambda h: h.tensor_tensor(out=v2(tt[:, 0, :]), in0=Lr, in1=tc, op=ALU.mult), reads=[ll_lb[0]] + tlb, writes=[tt_lb[0]])
                P.op("dve", lambda h: h.tensor_tensor(out=v2(tt[:, 1, :]), in0=Li, in1=ts_, op=ALU.mult), reads=[ll_lb[1]] + tlb, writes=[tt_lb[1]])
                P.op("dve", lambda h: h.tensor_tensor(out=v2(tt[:, 2, :]), in0=Li, in1=tc, op=ALU.mult), reads=[ll_lb[1]] + tlb, writes=[tt_lb[2]])
                P.op("dve", lambda h: h.tensor_tensor(out=v2(tt[:, 3, :]), in0=Lr, in1=ts_, op=ALU.mult), reads=[ll_lb[0]] + tlb, writes=[tt_lb[3]])
                hs = i % 2
                P.op("dve", lambda h: h.tensor_tensor(out=hb[:, hs, 0, :], in0=tt[:, 0, :], in1=tt[:, 1, :], op=ALU.subtract), reads=[tt_lb[0], tt_lb[1]], writes=[hb_lb[hs]])
                P.op("dve", lambda h: h.tensor_tensor(out=hb[:, hs, 1, :], in0=tt[:, 2, :], in1=tt[:, 3, :], op=ALU.add), reads=[tt_lb[2], tt_lb[3]], writes=[hb_lb[hs]])
                if is_sample:
                    P.op("dve", lambda h: h.tensor_tensor(out=h0t[:, 0, i, :], in0=v2(tt[:, 0, :])[:, :, 7], in1=v2(tt[:, 1, :])[:, :, 7], op=ALU.subtract),
                         reads=[tt_lb[0], tt_lb[1], S], writes=[S])
                    P.op("dve", lambda h: h.tensor_tensor(out=h0t[:, 1, i, :], in0=v2(tt[:, 2, :])[:, :, 7], in1=v2(tt[:, 3, :])[:, :, 7], op=ALU.add),
                         reads=[tt_lb[2], tt_lb[3], S], writes=[S])
                else:
                    P.op("dve", lambda h: h.tensor_tensor(out=car[:, 0, i:i + 1], in0=tt[:, 0, N - 1:N], in1=tt[:, 1, N - 1:N], op=ALU.subtract),
                         reads=[tt_lb[0], tt_lb[1], car_lb], writes=[car_lb])
                    P.op("dve", lambda h: h.tensor_tensor(out=car[:, 1, i:i + 1], in0=tt[:, 2, N - 1:N], in1=tt[:, 3, N - 1:N], op=ALU.add),
                         reads=[tt_lb[2], tt_lb[3], car_lb], writes=[car_lb])
                py = ps[:, 4, :N]
                P.op("pe", lambda h: h.matmul(py, lhsT=Cpr[:, i, :], rhs=hb[:, hs, 0, :], start=(q == 0), stop=False),
                     reads=[S, hb_lb[hs]], writes=[ps_lb[4]], inc=False)
                P.op("pe", lambda h: h.matmul(py, lhsT=Cni[:, i, :], rhs=hb[:, hs, 1, :], start=False, stop=(q == 3)),
                     reads=[S, hb_lb[hs]], writes=[ps_lb[4]], inc=True)
                if q == 3:
                    w0, w0_lb = ws.get(t_in[j // 4])
                    pu = ps[:, 5, :N]
                    for k in range(KT):
                        P.op("pe", lambda h, k=k: h.matmul(pu, lhsT=w0[:, k, (j % 4) * 128:(j % 4 + 1) * 128], rhs=xb[:, k, :N], start=(k == 0), stop=(k == KT - 1)),
                             reads=[w0_lb, xb_lb[k]], writes=[ps_lb[5]], inc=(k == KT - 1))
                    P.op("act", lambda h: h.copy(out=gts[0], in_=py), reads=[ps_lb[4]], writes=[gt_lb[0]])
                    P.op("dve", lambda h: h.scalar_tensor_tensor(out=gts[1], in0=pu, scalar=d_sb[:, j:j + 1], in1=gts[0], op0=ALU.mult, op1=ALU.add),
                         reads=[ps_lb[5], gt_lb[0], S], writes=[gt_lb[1]])
                    P.op("act", lambda h: h.activation(out=gts[0], in_=gts[1], func=AF.Square), reads=[gt_lb[1]], writes=[gt_lb[0]])
                    P.op("dve", lambda h: h.tensor_scalar(out=gts[2], in0=gts[0], scalar1=0.044715, scalar2=1.0, op0=ALU.mult, op1=ALU.add),
                         reads=[gt_lb[0]], writes=[gt_lb[2]])
                    P.op("dve", lambda h: h.tensor_tensor(out=gts[0], in0=gts[2], in1=gts[1], op=ALU.mult), reads=[gt_lb[2], gt_lb[1]], writes=[gt_lb[0]])
                    P.op("act", lambda h: h.activation(out=gts[2], in_=gts[0], func=AF.Sigmoid, scale=2.0 * math.sqrt(2.0 / math.pi)),
                         reads=[gt_lb[0]], writes=[gt_lb[2]])
                    P.op("dve", lambda h, j=j: h.tensor_tensor(out=yb[:, j, :], in0=gts[1], in1=gts[2], op=ALU.mult), reads=[gt_lb[1], gt_lb[2]], writes=[yb_lb[j]])

            vb, vb_lb = uT, uT_lb

            def glu_evac(o, po, plb):
                P.op("act", lambda h: h.activation(out=gts[0], in_=po, func=AF.Sigmoid, bias=bglu_sb[:, o:o + 1], scale=1.0), reads=[plb, S], writes=[gt_lb[0]])
                P.op("dve", lambda h: h.tensor_tensor(out=vb[:, o, :], in0=yb[:, o, :], in1=gts[0], op=ALU.mult), reads=[yb_lb[o], gt_lb[0]], writes=[vb_lb[o]])
            linear(t_glu, yb, yb_lb, glu_evac)

            def out_evac(o, po, plb):
                P.op("dve", lambda h: h.scalar_tensor_tensor(out=x32[:, o, :N], in0=po, scalar=1.0, in1=x32[:, o, :N], op0=ALU.mult, op1=ALU.add),
                     reads=[plb, x32_lb[o]], writes=[x32_lb[o]])
            linear(t_out, vb, vb_lb, out_evac)
            layer_norm(4, N)
            P.barrier()

    def ssm_finish():
        with contextlib.ExitStack() as fs:
            o1 = sb("o1", [128, 4, 32], F32, fs)
            po_ = sb("po_", [128, 64], F32, fs)
            sop("dve", lambda h: h.tensor_tensor(out=o1[:, 0], in0=car[:, 0], in1=f_re[:], op=ALU.mult))
            sop("dve", lambda h: h.tensor_tensor(out=o1[:, 1], in0=car[:, 1], in1=f_im[:], op=ALU.mult))
            sop("dve", lambda h: h.tensor_tensor(out=o1[:, 2], in0=car[:, 0], in1=f_im[:], op=ALU.mult))
            sop("dve", lambda h: h.tensor_tensor(out=o1[:, 3], in0=car[:, 1], in1=f_re[:], op=ALU.mult))
            sop("dve", lambda h: h.tensor_tensor(out=po_[:, 0:32], in0=o1[:, 0], in1=o1[:, 1], op=ALU.subtract))
            sop("dve", lambda h: h.tensor_tensor(out=po_[:, 32:64], in0=o1[:, 2], in1=o1[:, 3], op=ALU.add))
            P.dma("sp", pc_o[:, :], po_[:], reads=[S], sem="fo")
            if cfg["sample"]:
                o2 = sb("o2", [128, 4, 32, 16], F32, fs)
                so_ = sb("so_", [128, 2, 32, 16], F32, fs)
                frb = f_re[:, :].unsqueeze(2).to_broadcast([128, 32, 16])
                fib = f_im[:, :].unsqueeze(2).to_broadcast([128, 32, 16])
                sop("dve", lambda h: h.tensor_tensor(out=o2[:, 0], in0=h0t[:, 0], in1=frb, op=ALU.mult))
                sop("dve", lambda h: h.tensor_tensor(out=o2[:, 1], in0=h0t[:, 1], in1=fib, op=ALU.mult))
                sop("dve", lambda h: h.tensor_tensor(out=o2[:, 2], in0=h0t[:, 0], in1=fib, op=ALU.mult))
                sop("dve", lambda h: h.tensor_tensor(out=o2[:, 3], in0=h0t[:, 1], in1=frb, op=ALU.mult))
                sop("dve", lambda h: h.tensor_tensor(out=so_[:, 0], in0=o2[:, 0], in1=o2[:, 1], op=ALU.subtract))
                sop("dve", lambda h: h.tensor_tensor(out=so_[:, 1], in0=o2[:, 2], in1=o2[:, 3], op=ALU.add))
                P.dma("sp", sc_o[:, :], so_[:].rearrange("p a b c -> p (a b c)"), reads=[S], sem="fo")

    def load_unit(src, t0, N):
        v = src.rearrange("(k p) t -> p k t", p=128)
        for k in range(KT):
            q = k % 2
            P.dma("sp", ttmp[:, q, :N], v[:, k, t0:t0 + N], writes=[ttmp_lb[q]], sem=f"xi{q}")
            P.op("act", lambda h, k=k, q=q: h.activation(out=x32[:, k, :N], in_=ttmp[:, q, :N], func=AF.Copy, scale=ALPHA),
                 reads=[ttmp_lb[q]], writes=[x32_lb[k]])
            P.op("dve", lambda h, k=k, q=q: h.tensor_copy(out=xb[:, k, :N], in_=ttmp[:, q, :N]),
                 reads=[ttmp_lb[q]], writes=[xb_lb[k]])

    def run_unit(src, dst, t0, N, is_sample, u):
        load_unit(src, t0, N)
        dv = dst.rearrange("(k p) t -> p k t", p=128)
        fo = lambda k: dv[:, k, t0:t0 + N]
        if stage <= 1:
            ffn(0, 0, N, final_out=fo)
            return
        ffn(0, 0, N)
        if stage == 2:
            attention(N, t0, is_sample)
            ffn(1, 2, N, final_out=fo)
            return
        attention(N, t0, is_sample)
        ffn(1, 2, N)
        ffn(2, 3, N)
        ssm(N, is_sample)
        ffn(3, 5, N, final_out=fo)

    for u in range(n_units):
        run_unit(xpT, ypT, u * UT, UT, False, u)
    if cfg["sample"]:
        run_unit(xsT, ysT, 0, 128, True, 0)
    if stage >= 3:
        ssm_finish()
    P.barrier()
    print("instructions:", P.ninst, {e: P.cnt[e] for e in P.cnt})
    return nc, P


def host_inputs(inputs, c):
    f = lambda a: np.ascontiguousarray(a, dtype=np.float32)
    s = c % 4
    m = {}
    m["xpT"] = f(inputs["x_prompt"][s].T)
    m["xsT"] = f(inputs["x_sample"][16 * c:16 * c + 16].reshape(128, D).T)
    m["wg"] = f(inputs["ffn_w_gate"].reshape(4, D, DFF))
    m["wu"] = f(inputs["ffn_w_up"].reshape(4, D, DFF))
    m["wd"] = f(inputs["ffn_w_down"].reshape(4, DFF, D))
    m["w_in"] = f(inputs["attn_w_in"][0])
    m["w_out"] = f(inputs["attn_w_out"][0])
    inv = 10000.0 ** (-np.arange(32, dtype=np.float32) / 32)
    fr = np.tile(inv, 4)[:, None].astype(np.float32)
    pos = np.arange(SEQ, dtype=np.float32)[None, :]
    m["ropec"] = np.cos(fr * pos).astype(np.float32)
    m["ropes"] = np.sin(fr * pos).astype(np.float32)
    poss = np.tile(16384.0 + np.arange(8, dtype=np.float32), 16)[None, :]
    m["ropecs"] = np.cos(fr * poss).astype(np.float32)
    m["ropess"] = np.sin(fr * poss).astype(np.float32)

    def mult(dl):
        dl = np.asarray(dl)
        return ((dl >= 0) & (dl <= 128)).astype(np.float32) + ((dl >= 0) & (dl <= 512) & (dl % 4 == 0)) + ((dl >= 0) & (dl <= 2048) & (dl % 16 == 0))
    j = np.arange(128)[:, None, None]
    i = np.arange(128)[None, None, :]
    b = np.arange(17)[None, :, None]
    m["mA"] = f(mult(128 * b + i - j).reshape(128, 17 * 128))
    dB = 128 * np.arange(2)[None, :, None] + i - j
    m["mB"] = f(((dB >= 0) & (dB <= 127)).reshape(128, 2 * 128))
    i8 = np.arange(8)[None, None, :]
    ms = mult(2048 + i8 - (128 * b + j)).astype(np.float32)
    ms[:, 16, :] = 0
    ms[:8, 16, :] = mult(i8[0] - np.arange(8)[:, None])
    m["msA"] = f(ms.reshape(128, 17 * 8))
    mb = np.zeros((128, 2, 8), np.float32)
    d0 = 128 + i8[0] - np.arange(128)[:, None]
    mb[:, 0, :] = (d0 >= 0) & (d0 <= 127)
    d1 = i8[0] - np.arange(8)[:, None]
    mb[:8, 1, :] = (d1 >= 0)
    m["msB"] = f(mb.reshape(128, 16))
    m["sinks"] = f(np.broadcast_to(inputs["attn_sinks"][0][None, :], (128, 8)))
    sl = slice(16 * c, 16 * c + 16)
    m["cakT"] = f(inputs["cache_a_k"][0, sl].reshape(16, 2048, 512).transpose(0, 2, 1))
    m["cav"] = f(inputs["cache_a_v"][0, sl].reshape(16, 2048, 512))
    m["cbkT"] = f(inputs["cache_b_k"][0, sl].reshape(16, 128, 128).transpose(0, 2, 1))
    m["cbv"] = f(inputs["cache_b_v"][0, sl].reshape(16, 128, 128))
    m["s_win"] = f(inputs["ssm_w_in"][0])
    m["s_wglu"] = f(inputs["ssm_w_glu"][0])
    m["s_wout"] = f(inputs["ssm_w_out"][0])
    st = lambda a: f(np.asarray(a).reshape(32, 2, 64).transpose(1, 2, 0).reshape(128, 32))
    m["s_lre"] = st(inputs["ssm_lambda_re"][0])
    m["s_lim"] = st(inputs["ssm_lambda_im"][0])
    m["s_ldt"] = st(np.broadcast_to(inputs["ssm_log_dt"][0][:, None], (64, 64)))
    def btab(b):
        t = np.zeros((128, 32, 128), np.float32)
        for i in range(32):
            for gg in range(2):
                r0 = 32 * (i % 4) + 16 * gg
                t[r0:r0 + 16, i, 64 * gg:64 * gg + 64] = b[2 * i + gg].T
        return t.reshape(128, 4096)
    def ctab(cc):
        t = np.zeros((128, 32, 128), np.float32)
        for i in range(32):
            for gg in range(2):
                c0 = 32 * (i % 4) + 16 * gg
                t[64 * gg:64 * gg + 64, i, c0:c0 + 16] = cc[2 * i + gg].T
        return t.reshape(128, 4096)
    m["s_Bre"] = btab(inputs["ssm_b_re"][0]); m["s_Bim"] = btab(inputs["ssm_b_im"][0])
    m["s_Cre"] = ctab(inputs["ssm_c_re"][0]); m["s_Cim"] = ctab(inputs["ssm_c_im"][0])
    m["s_d"] = f(inputs["ssm_d"][0].reshape(8, 128).T)
    m["s_bglu"] = f(inputs["ssm_b_glu"][0].reshape(8, 128).T)
    m["tau1"] = f(np.broadcast_to(np.arange(1, 513, dtype=np.float32)[None, :], (128, 512)))
    sm = np.ones((128, 128), np.float32); sm[:, ::8] = 0
    m["smask"] = sm
    stt = lambda a: f(np.asarray(a).reshape(16, 32, 2, 64).transpose(2, 3, 1, 0).reshape(128, 512))
    m["st_re"] = stt(inputs["state_c_re"][0, 16 * c:16 * c + 16]); m["st_im"] = stt(inputs["state_c_im"][0, 16 * c:16 * c + 16])
    m["lng"] = f(inputs["ln_g"].reshape(6, KT, 128).transpose(2, 0, 1).reshape(128, 48))
    m["lnb"] = f(inputs["ln_b"].reshape(6, KT, 128).transpose(2, 0, 1).reshape(128, 48))
    return m


def run(inputs, cfg=None, trace=False):
    cfg = dict(CFG) if cfg is None else cfg
    nc, P = build(cfg)
    shared = {}
    in_maps = []
    for c in range(8):
        m = host_inputs(inputs, c)
        for k in ("wg", "wu", "wd", "lng", "lnb", "w_in", "w_out", "ropec", "ropes", "ropecs", "ropess", "mA", "mB", "msA", "msB", "sinks",
                  "s_win", "s_wglu", "s_wout", "s_lre", "s_lim", "s_ldt", "s_Bre", "s_Bim", "s_Cre", "s_Cim", "s_d", "s_bglu", "tau1", "smask"):
            if k in shared:
                m[k] = shared[k]
            else:
                shared[k] = m[k]
        in_maps.append(m)
    if not cfg["sample"]:
        in_maps = [{k: v for k, v in m.items() if k not in ("cakT", "cav", "cbkT", "cbv")} for m in in_maps]
    if cfg["stage"] < 3:
        in_maps = [{k: v for k, v in m.items() if not (k.startswith("s_") or k in ("tau1", "smask", "st_re", "st_im"))} for m in in_maps]
    res = run_bass_kernel_spmd(nc, in_maps, core_ids=list(range(8)), trace=trace)
    return res


def kernel(**inputs):
    inputs = {k: np.asarray(v) for k, v in inputs.items()}
    res = run(inputs)
    r = res.results
    yp = np.stack([r[s]["ypT"].T for s in range(4)], 0)
    ys = np.concatenate([r[c]["ysT"].T.reshape(16, 8, D) for c in range(8)], 0)
    pak = np.stack([r[s]["pakT"].T.reshape(2048, 8, 64) for s in range(4)], 0)[None]
    pav = np.stack([r[s]["pav"].reshape(2048, 8, 64) for s in range(4)], 0)[None]
    pbk = np.stack([r[s]["pbkT"].T.reshape(128, 2, 64) for s in range(4)], 0)[None]
    pbv = np.stack([r[s]["pbv"].reshape(128, 2, 64) for s in range(4)], 0)[None]
    sak = np.concatenate([r[c]["sakT"].T.reshape(16, 8, 8, 64) for c in range(8)], 0)[None]
    sav = np.concatenate([r[c]["sav"].reshape(16, 8, 8, 64) for c in range(8)], 0)[None]
    sbk = np.concatenate([r[c]["sbkT"].T.reshape(16, 8, 2, 64) for c in range(8)], 0)[None]
    sbv = np.concatenate([r[c]["sbv"].reshape(16, 8, 2, 64) for c in range(8)], 0)[None]
    unst = lambda a: a.reshape(2, 64, 32).transpose(2, 0, 1).reshape(64, 64)
    pcr = np.stack([unst(r[s]["pc"][:, 0:32]) for s in range(4)], 0)[None]
    pci = np.stack([unst(r[s]["pc"][:, 32:64]) for s in range(4)], 0)[None]
    unss = lambda a: a.reshape(2, 64, 32, 16).transpose(3, 2, 0, 1).reshape(16, 64, 64)
    scr = np.concatenate([unss(r[c]["sc"][:, 0:512]) for c in range(8)], 0)[None]
    sci = np.concatenate([unss(r[c]["sc"][:, 512:1024]) for c in range(8)], 0)[None]
    out = (yp, ys, pak, pav, pbk, pbv, pcr, pci, sak, sav, sbk, sbv, scr, sci)
    return tuple(np.ascontiguousarray(o, dtype=np.float32) for o in out)
```

```python
import contextlib
import math
import numpy as np
import concourse.bass as bass
import concourse.mybir as mybir
from concourse.bass_utils import run_bass_kernel_spmd

F32 = mybir.dt.float32
BF16 = mybir.dt.bfloat16
AF = mybir.ActivationFunctionType
ALU = mybir.AluOpType

D = 1024
KT = 8
DFF = 2816
NFF = 22
SEQ = 4096
UT = 512
NUNIT = SEQ // UT
ALPHA = 2.0 ** 0.5
EPS = 1e-5
NSLOT = 3

CFG = {"n_units": NUNIT, "stage": 99, "sample": True}


class LB:
    __slots__ = ("w", "r")

    def __init__(self):
        self.w = None
        self.r = {}


def lbs(n):
    return [LB() for _ in range(n)]


class Prog:
    def __init__(self, nc):
        self.nc = nc
        self.es = contextlib.ExitStack()
        self.h = {"pe": nc.tensor, "act": nc.scalar, "dve": nc.vector, "pool": nc.gpsimd, "sp": nc.sync}
        self.sem = {e: self.es.enter_context(nc.semaphore("E_" + e)) for e in self.h}
        self.cnt = {e: 0 for e in self.h}
        self.waited = {e: {} for e in self.h}
        self.dsem = {}
        self.ninst = 0
        names = ["c", "c2", "xi0", "xi1", "yo0", "yo1", "rp", "ko0", "ko1", "vo0", "vo1", "sc", "sc2", "tws0", "tws1", "twc0", "twc1",
                 "tl0", "tl1", "fo", "ck0", "ck1", "cv0", "cv1", "cb0", "cb1", "vq0", "vq1"] + [f"w{i}" for i in range(NSLOT)]
        for n in names:
            self.dsem[n] = [self.es.enter_context(nc.semaphore("D_" + n)), 0]

    def _wait(self, e, ev):
        sem, val, src = ev
        if src == e and e == "pe":
            return
        k = id(sem)
        if self.waited[e].get(k, 0) >= val:
            return
        self.h[e].wait_ge(sem, val)
        self.waited[e][k] = val

    def _deps(self, e, reads, writes):
        for b in reads:
            if b.w is not None:
                self._wait(e, b.w)
        for b in writes:
            if b.w is not None:
                self._wait(e, b.w)
            for ev in b.r.values():
                self._wait(e, ev)

    def _post(self, ev, reads, writes):
        k = id(ev[0])
        for b in reads:
            o = b.r.get(k)
            if o is None or o[1] < ev[1]:
                b.r[k] = ev
        for b in writes:
            b.w = ev
            b.r = {}

    def op(self, e, fn, reads=(), writes=(), inc=True):
        self._deps(e, reads, writes)
        ins = fn(self.h[e])
        self.ninst += 1
        if inc:
            self.cnt[e] += 1
            ins.then_inc(self.sem[e], 1)
            ev = (self.sem[e], self.cnt[e], e)
        else:
            ev = (self.sem[e], self.cnt[e] + 1, e)
        self._post(ev, reads, writes)
        return ins

    def dma(self, q, out, in_, reads=(), writes=(), sem="d"):
        self._deps(q, reads, writes)
        if sem not in self.dsem:
            self.dsem[sem] = [self.es.enter_context(self.nc.semaphore("D_" + sem)), 0]
        d = self.dsem[sem]
        d[1] += 16
        self.h[q].dma_start(out=out, in_=in_).then_inc(d[0], 16)
        self.ninst += 1
        ev = (d[0], d[1], None)
        self._post(ev, reads, writes)

    def barrier(self):
        for e in self.h:
            for o in self.h:
                if o != e and self.cnt[o] > 0:
                    self._wait(e, (self.sem[o], self.cnt[o], o))
            for d in self.dsem.values():
                if d[1] > 0:
                    self._wait(e, (d[0], d[1], None))


def build(cfg):
    nc = bass.Bass("TRN2", target_bir_lowering=False)
    P = Prog(nc)
    es = P.es
    n_units = cfg["n_units"]
    stage = cfg["stage"]

    def din(name, shape, dt=F32):
        return nc.dram_tensor(name, list(shape), dt, kind="ExternalInput").ap()

    def dout(name, shape, dt=F32):
        return nc.dram_tensor(name, list(shape), dt, kind="ExternalOutput").ap()

    uid = [0]

    def sb(name, shape, dt, stack=es):
        uid[0] += 1
        return stack.enter_context(nc.sbuf_tensor(f"{name}_{uid[0]}", list(shape), dt))

    xpT = din("xpT", [D, SEQ])
    xsT = din("xsT", [D, 128])
    wg = din("wg", [4, D, DFF])
    wu = din("wu", [4, D, DFF])
    wd = din("wd", [4, DFF, D])
    lng = din("lng", [128, 48])
    lnb = din("lnb", [128, 48])
    w_in = din("w_in", [D, 2304])
    w_out = din("w_out", [D, D])
    ropec = din("ropec", [128, SEQ])
    ropes = din("ropes", [128, SEQ])
    ropecs = din("ropecs", [128, 128])
    ropess = din("ropess", [128, 128])
    mA_d = din("mA", [128, 17 * 128])
    mB_d = din("mB", [128, 2 * 128])
    msA_d = din("msA", [128, 17 * 8])
    msB_d = din("msB", [128, 2 * 8])
    sinks_d = din("sinks", [128, 8])
    if cfg["sample"]:
        cakT = din("cakT", [16, 512, 2048])
        cav = din("cav", [16, 2048, 512])
        cbkT = din("cbkT", [16, 128, 128])
        cbv = din("cbv", [16, 128, 128])
    pakT = dout("pakT", [512, 2048])
    pav = dout("pav", [2048, 512])
    pbkT = dout("pbkT", [128, 128])
    pbv = dout("pbv", [128, 128])
    sakT = dout("sakT", [512, 128])
    sav = dout("sav", [128, 512])
    sbkT = dout("sbkT", [128, 128])
    sbv = dout("sbv", [128, 128])
    ypT = dout("ypT", [D, SEQ])
    ysT = dout("ysT", [D, 128])

    x32 = sb("x32", [128, KT, UT], F32)
    xb = sb("xb", [128, KT, UT], BF16)
    x32_lb = lbs(KT)
    xb_lb = lbs(KT)
    wring = [sb(f"wring{i}", [128, 4096], BF16) for i in range(NSLOT)]
    wring_lb = lbs(NSLOT)
    ps = es.enter_context(nc.psum_tensor("ps", [128, 8, 512], F32))
    ps_lb = lbs(8)
    ones_bf = sb("ones_bf", [128, 128], BF16)
    g_sb = sb("g_sb", [128, 48], F32)
    b_sb = sb("b_sb", [128, 48], F32)
    ga_sb = sb("ga_sb", [128, 48], F32)
    ba_sb = sb("ba_sb", [128, 48], F32)
    eps_sb = sb("eps_sb", [128, 1], F32)
    const_lb = LB()
    xsq = sb("xsq", [128, 2, UT], BF16)
    xsq_lb = lbs(2)
    st_mean = sb("st_mean", [128, UT], F32)
    st_a = sb("st_a", [128, UT], F32)
    st_rstd = sb("st_rstd", [128, UT], F32)
    st_lb = lbs(3)
    ttmp = sb("ttmp", [128, 2, UT], F32)
    ttmp_lb = lbs(2)
    ystage = sb("ystage", [128, 2, UT], F32)
    ystage_lb = lbs(2)

    NRB = 20
    KTs = sb("KTs", [128, 4, NRB * 128], BF16)
    Vst = sb("Vst", [128, NRB, 512], BF16)
    kt_lb = lbs(NRB)
    v_lb = lbs(NRB)
    KTB = sb("KTB", [128, 2, 8, 128], BF16)
    VB = sb("VB", [128, 8, 128], BF16)
    ktb_lb = lbs(8)
    vb_lb = lbs(8)
    mA = sb("mA", [128, 17, 128], BF16)
    mB = sb("mB", [128, 2, 128], BF16)
    msA = sb("msA", [128, 17, 8], BF16)
    msB = sb("msB", [128, 2, 8], BF16)
    sink_e = sb("sink_e", [128, 8], F32)
    zrow = sb("zrow", [1, 512], BF16)
    P.dma("pool", mA[:], mA_d.rearrange("p (a b) -> p a b", a=17), writes=[const_lb], sem="c2")
    P.dma("pool", mB[:], mB_d.rearrange("p (a b) -> p a b", a=2), writes=[const_lb], sem="c2")
    P.dma("pool", msA[:], msA_d.rearrange("p (a b) -> p a b", a=17), writes=[const_lb], sem="c2")
    P.dma("pool", msB[:], msB_d.rearrange("p (a b) -> p a b", a=2), writes=[const_lb], sem="c2")
    P.dma("sp", sink_e[:], sinks_d[:, :], writes=[const_lb], sem="c")
    P.op("act", lambda h: h.activation(out=sink_e[:], in_=sink_e[:], func=AF.Exp), reads=[const_lb], writes=[const_lb])
    P.op("dve", lambda h: h.memset(zrow[:], 0.0), writes=[const_lb])

    P.op("dve", lambda h: h.memset(ones_bf[:], 1.0), writes=[const_lb])
    P.op("dve", lambda h: h.memset(eps_sb[:], EPS), writes=[const_lb])
    P.dma("sp", g_sb[:], lng[:, :], writes=[const_lb], sem="c")
    P.dma("sp", b_sb[:], lnb[:, :], writes=[const_lb], sem="c")
    P.op("act", lambda h: h.mul(ga_sb[:], g_sb[:], ALPHA), reads=[const_lb], writes=[const_lb])
    P.op("act", lambda h: h.mul(ba_sb[:], b_sb[:], ALPHA), reads=[const_lb], writes=[const_lb])

    P.barrier()

    class WStream:
        def __init__(self):
            self.items = []
            self.issued = 0
            self.total = 0

        def add(self, dram_ap, shape):
            self.items.append((dram_ap, shape))
            return len(self.items) - 1

        def _issue(self, i):
            dram_ap, shape = self.items[i]
            gi = self.base + i
            s = gi % NSLOT
            n = int(np.prod(shape))
            view = wring[s][:, 0:n]
            if len(shape) == 2:
                view = view.rearrange("p (a b) -> p a b", a=shape[0])
            if dram_ap.shape[0] == 64:
                view = view[0:64]
            P.dma("pool", view, dram_ap, writes=[wring_lb[s]], sem=f"w{s}")

        def start(self, base):
            self.base = base
            self.issued = 0

        def get(self, i):
            while self.issued < len(self.items) and self.issued <= i + NSLOT - 2:
                self._issue(self.issued)
                self.issued += 1
            gi = self.base + i
            s = gi % NSLOT
            _, shape = self.items[i]
            n = int(np.prod(shape))
            view = wring[s][:, 0:n]
            if len(shape) == 2:
                view = view.rearrange("p (a b) -> p a b", a=shape[0])
            return view, wring_lb[s]

    wbase = [0]

    def new_stream():
        w = WStream()
        return w

    def start_stream(w):
        w.start(wbase[0])
        wbase[0] += len(w.items)

    def layer_norm(lj, N, final_out=None):
        inv = 1.0 / D
        s1, s2 = ps[:, 6, :N], ps[:, 7, :N]
        for k in range(KT):
            q = k % 2
            P.op("act", lambda h, k=k, q=q: h.activation(out=xsq[:, q, :N], in_=x32[:, k, :N], func=AF.Square),
                 reads=[x32_lb[k]], writes=[xsq_lb[q]])
            P.op("dve", lambda h, k=k: h.tensor_copy(out=xb[:, k, :N], in_=x32[:, k, :N]),
                 reads=[x32_lb[k]], writes=[xb_lb[k]])
            P.op("pe", lambda h, k=k: h.matmul(s1, lhsT=ones_bf[:], rhs=xb[:, k, :N], start=(k == 0), stop=(k == KT - 1)),
                 reads=[xb_lb[k], const_lb], writes=[ps_lb[6]], inc=(k == KT - 1))
            P.op("pe", lambda h, k=k, q=q: h.matmul(s2, lhsT=ones_bf[:], rhs=xsq[:, q, :N], start=(k == 0), stop=(k == KT - 1)),
                 reads=[xsq_lb[q], const_lb], writes=[ps_lb[7]], inc=True)
        P.op("act", lambda h: h.activation(out=st_mean[:, :N], in_=s1, func=AF.Copy, scale=inv),
             reads=[ps_lb[6]], writes=[st_lb[0]])
        P.op("act", lambda h: h.activation(out=st_a[:, :N], in_=s1, func=AF.Square, scale=inv),
             reads=[ps_lb[6]], writes=[st_lb[1]])
        P.op("dve", lambda h: h.scalar_tensor_tensor(out=st_a[:, :N], in0=s2, scalar=inv, in1=st_a[:, :N],
                                                     op0=ALU.mult, op1=ALU.subtract),
             reads=[ps_lb[7], st_lb[1]], writes=[st_lb[1]])
        P.op("act", lambda h: h.activation(out=st_a[:, :N], in_=st_a[:, :N], func=AF.Sqrt, bias=eps_sb[:, 0:1], scale=1.0),
             reads=[st_lb[1], const_lb], writes=[st_lb[1]])
        P.op("dve", lambda h: h.reciprocal(out=st_rstd[:, :N], in_=st_a[:, :N]),
             reads=[st_lb[1]], writes=[st_lb[2]])
        for k in range(KT):
            q = k % 2
            c = lj * KT + k
            P.op("dve", lambda h, k=k, q=q: h.tensor_tensor(out=ttmp[:, q, :N], in0=x32[:, k, :N], in1=st_mean[:, :N], op=ALU.subtract),
                 reads=[x32_lb[k], st_lb[0]], writes=[ttmp_lb[q]])
            P.op("dve", lambda h, q=q: h.tensor_tensor(out=ttmp[:, q, :N], in0=ttmp[:, q, :N], in1=st_rstd[:, :N], op=ALU.mult),
                 reads=[ttmp_lb[q], st_lb[2]], writes=[ttmp_lb[q]])
            P.op("act", lambda h, k=k, q=q, c=c: h.activation(out=xb[:, k, :N], in_=ttmp[:, q, :N], func=AF.Identity,
                                                               scale=g_sb[:, c:c + 1], bias=b_sb[:, c:c + 1]),
                 reads=[ttmp_lb[q], const_lb], writes=[xb_lb[k]])
            if final_out is None:
                P.op("act", lambda h, k=k, q=q, c=c: h.activation(out=x32[:, k, :N], in_=ttmp[:, q, :N], func=AF.Identity,
                                                                   scale=ga_sb[:, c:c + 1], bias=ba_sb[:, c:c + 1]),
                     reads=[ttmp_lb[q], const_lb], writes=[x32_lb[k]])
            else:
                P.op("act", lambda h, k=k, q=q, c=c: h.activation(out=ystage[:, q, :N], in_=ttmp[:, q, :N], func=AF.Identity,
                                                                   scale=g_sb[:, c:c + 1], bias=b_sb[:, c:c + 1]),
                     reads=[ttmp_lb[q], const_lb], writes=[ystage_lb[q]])
                P.dma("sp", final_out(k), ystage[:, q, :N], reads=[ystage_lb[q]], sem=f"yo{q}")

    def ffn(fi, lj, N, final_out=None):
        with contextlib.ExitStack() as fs:
            hbuf = sb("hbuf", [128, NFF, UT], BF16, fs)
            hbuf_lb = lbs(NFF)
            sil = sb("sil", [128, 2, UT], F32, fs)
            sil_lb = lbs(2)
            wgv = wg[fi].rearrange("(k p) n -> p k n", p=128)
            wuv = wu[fi].rearrange("(k p) n -> p k n", p=128)
            wdv = wd[fi].rearrange("(k p) n -> p k n", p=128)
            ws = new_stream()
            groups = []
            for g in range(6):
                gw = 512 if g < 5 else 256
                ig = ws.add(wgv[:, :, g * 512:g * 512 + gw], (KT, gw))
                iu = ws.add(wuv[:, :, g * 512:g * 512 + gw], (KT, gw))
                groups.append((ig, iu, gw))
            dts = [ws.add(wdv[:, :, o * 128:(o + 1) * 128], (NFF, 128)) for o in range(KT)]
            start_stream(ws)
            for g, (ig, iu, gw) in enumerate(groups):
                sg, sg_lb = ws.get(ig)
                su, su_lb = ws.get(iu)
                for c in range(gw // 128):
                    ff = 4 * g + c
                    q = ff % 2
                    pg, pu = ps[:, q, :N], ps[:, 2 + q, :N]
                    for k in range(KT):
                        P.op("pe", lambda h, k=k, c=c, sg=sg, pg=pg: h.matmul(pg, lhsT=sg[:, k, c * 128:(c + 1) * 128], rhs=xb[:, k, :N],
                                                                 start=(k == 0), stop=(k == KT - 1)),
                             reads=[sg_lb, xb_lb[k]], writes=[ps_lb[q]], inc=(k == KT - 1))
                    for k in range(KT):
                        P.op("pe", lambda h, k=k, c=c, su=su, pu=pu: h.matmul(pu, lhsT=su[:, k, c * 128:(c + 1) * 128], rhs=xb[:, k, :N],
                                                                 start=(k == 0), stop=(k == KT - 1)),
                             reads=[su_lb, xb_lb[k]], writes=[ps_lb[2 + q]], inc=(k == KT - 1))
                    P.op("act", lambda h, q=q, pg=pg: h.activation(out=sil[:, q, :N], in_=pg, func=AF.Silu),
                         reads=[ps_lb[q]], writes=[sil_lb[q]])
                    P.op("dve", lambda h, q=q, ff=ff, pu=pu: h.tensor_tensor(out=hbuf[:, ff, :N], in0=sil[:, q, :N], in1=pu, op=ALU.mult),
                         reads=[sil_lb[q], ps_lb[2 + q]], writes=[hbuf_lb[ff]])
            for o in range(KT):
                sd, sd_lb = ws.get(dts[o])
                b = 4 + (o % 2)
                po = ps[:, b, :N]
                for k in range(NFF):
                    P.op("pe", lambda h, k=k, sd=sd, po=po: h.matmul(po, lhsT=sd[:, k, :], rhs=hbuf[:, k, :N],
                                                                     start=(k == 0), stop=(k == NFF - 1)),
                         reads=[sd_lb, hbuf_lb[k]], writes=[ps_lb[b]], inc=(k == NFF - 1))
                P.op("dve", lambda h, o=o, po=po: h.scalar_tensor_tensor(out=x32[:, o, :N], in0=po, scalar=0.5, in1=x32[:, o, :N],
                                                                         op0=ALU.mult, op1=ALU.add),
                     reads=[ps_lb[b], x32_lb[o]], writes=[x32_lb[o]])
            layer_norm(lj, N, final_out)
            P.barrier()


    SC = 0.125

    def attention(N, t0, is_sample):
        nb = N // 128
        gb0 = t0 // 128
        with contextlib.ExitStack() as fs:
            QA = sb("QA", [128, 8, N], BF16, fs)
            QB = sb("QB", [128, 8, N], BF16, fs)
            qa_lb, qb_lb = LB(), LB()
            P.op("dve", lambda h: h.memset(QA[:], 0.0), writes=[qa_lb])
            P.op("dve", lambda h: h.memset(QB[:], 0.0), writes=[qb_lb])
            wrot = sb("wrot", [128, 4096], BF16, fs)
            wrot_lb = LB()
            wdup = wrot[:, 0:2048].rearrange("p (k v c) -> p k v c", k=KT, v=2)
            wdupr = wrot[:, 2048:4096].rearrange("p (k v c) -> p k v c", k=KT, v=2)
            wdup_lb = wrot_lb
            rc = sb("rc", [128, N], F32, fs)
            rs = sb("rs", [128, N], F32, fs)
            rope_lb = LB()
            r1, r1_lb = ttmp, ttmp_lb
            r2, r2_lb = ystage, ystage_lb
            PT = sb("PT", [128, 2, 512], BF16, fs)
            pt_lb = lbs(2)
            oT = sb("oT", [64, 16, N], BF16, fs)
            ot_lb = LB()
            rd = sb("rd", [64, 512], F32, fs)
            rd_lb = LB()
            vstg, vstg_lb = ystage, ystage_lb
            if is_sample:
                KTn = sb("KTn", [128, 4, 128], BF16, fs)
                KTBn = sb("KTBn", [128, 2, 128], BF16, fs)
                Vn = sb("Vn", [128, 640], BF16, fs)
                Vsq = sb("Vsq", [8, 2, 640], BF16, fs)
                vsq_lb = lbs(2)
                KBs = sb("KBs", [128, 2, 2, 128], BF16, fs)
                VBs = sb("VBs", [128, 2, 128], BF16, fs)
                kbs_lb = lbs(2)
                ktn_lb = LB()
            csrc, ssrc = (ropecs, ropess) if is_sample else (ropec, ropes)
            P.dma("sp", rc[:], csrc[:, t0:t0 + N], writes=[rope_lb], sem="rp")
            P.dma("sp", rs[:], ssrc[:, t0:t0 + N], writes=[rope_lb], sem="rp")
            wv = w_in.rearrange("(k p) n -> p k n", p=128)
            ws = new_stream()
            tiles = [ws.add(wv[:, :, i * 512:(i + 1) * 512], (KT, 512)) for i in range(4)]
            tiles.append(ws.add(wv[:, :, 2048:2304], (KT, 256)))
            wov = w_out.rearrange("(h d) n -> d h n", d=64)
            wo_t = [ws.add(wov[:, :, o * 128:(o + 1) * 128], (16, 128)) for o in range(KT)]
            start_stream(ws)
            cnt = [0]

            def make_rot(src, dst, src_lb):
                sv = src.rearrange("p (a t i) -> p a t i", t=2, i=32)
                dv = dst.rearrange("p (a t i) -> p a t i", t=2, i=32)
                P.op("act", lambda h: h.mul(dv[:, :, 0, :], sv[:, :, 1, :], -1.0), reads=[src_lb], writes=[wrot_lb])
                P.op("dve", lambda h: h.tensor_copy(out=dv[:, :, 1, :], in_=sv[:, :, 0, :]), reads=[src_lb], writes=[wrot_lb])

            def proj_rope(wslot, wslot_lb, rotv, rot_lb, col0, dst_bf, dst_lbs, out32=None, qz=None):
                i = cnt[0] % 2
                cnt[0] += 1
                pz, pr = ps[:, i, :N], ps[:, 2 + i, :N]
                for k in range(KT):
                    P.op("pe", lambda h, k=k: h.matmul(pz, lhsT=wslot[:, k, col0:col0 + 128], rhs=xb[:, k, :N], start=(k == 0), stop=(k == KT - 1)),
                         reads=[wslot_lb, xb_lb[k]], writes=[ps_lb[i]], inc=(k == KT - 1))
                for k in range(KT):
                    P.op("pe", lambda h, k=k: h.matmul(pr, lhsT=rotv[:, k, col0:col0 + 128], rhs=xb[:, k, :N], start=(k == 0), stop=(k == KT - 1)),
                         reads=[rot_lb, xb_lb[k]], writes=[ps_lb[2 + i]], inc=(k == KT - 1))
                P.op("dve", lambda h: h.tensor_tensor(out=r1[:, i, :N], in0=pz, in1=rc[:], op=ALU.mult), reads=[ps_lb[i], rope_lb], writes=[r1_lb[i]])
                P.op("dve", lambda h: h.tensor_tensor(out=r2[:, i, :N], in0=pr, in1=rs[:], op=ALU.mult), reads=[ps_lb[2 + i], rope_lb], writes=[r2_lb[i]])
                a1, a2 = r1[:, i, :N], r2[:, i, :N]
                if len(dst_bf.shape) == 3:
                    a1 = a1.rearrange("p (a b) -> p a b", b=128)
                    a2 = a2.rearrange("p (a b) -> p a b", b=128)
                if qz is not None:
                    Qt, hp_ = qz
                    P.op("dve", lambda h: h.tensor_tensor(out=Qt[0:64, 2 * hp_, :], in0=r1[0:64, i, :N], in1=r2[0:64, i, :N], op=ALU.add), reads=[r1_lb[i], r2_lb[i]], writes=dst_lbs)
                    P.op("dve", lambda h: h.tensor_tensor(out=Qt[64:128, 2 * hp_ + 1, :], in0=r1[64:128, i, :N], in1=r2[64:128, i, :N], op=ALU.add), reads=[r1_lb[i], r2_lb[i]], writes=dst_lbs)
                elif out32 is None:
                    P.op("dve", lambda h: h.tensor_tensor(out=dst_bf, in0=a1, in1=a2, op=ALU.add), reads=[r1_lb[i], r2_lb[i]], writes=dst_lbs)
                else:
                    dap, p0, p1, c0, c1 = out32
                    P.op("dve", lambda h: h.tensor_tensor(out=r1[:, i, :N], in0=r1[:, i, :N], in1=r2[:, i, :N], op=ALU.add), reads=[r1_lb[i], r2_lb[i]], writes=[r1_lb[i]])
                    P.op("act", lambda h: h.copy(out=dst_bf, in_=a1), reads=[r1_lb[i]], writes=dst_lbs)
                    P.dma("sp", dap, r1[p0:p1, i, c0:c1], reads=[r1_lb[i]], sem=f"ko{i}")

            sub = cfg.get("attn_sub", 9)
            s0, s0_lb = ws.get(tiles[0])
            make_rot(wring[(ws.base + tiles[0]) % NSLOT][:, 0:4096], wrot[:, 0:4096], s0_lb)
            rv = wrot[:, 0:4096].rearrange("p (a b) -> p a b", a=KT)
            for hp in range(4):
                proj_rope(s0, s0_lb, rv, wrot_lb, hp * 128, QA[:, hp, :], [qa_lb], qz=(QA, hp))
            if sub < 2:
                layer_norm(1, N); P.barrier(); return
            s1, s1_lb = ws.get(tiles[1])
            make_rot(wring[(ws.base + tiles[1]) % NSLOT][:, 0:4096], wrot[:, 0:4096], s1_lb)
            for hp in range(4):
                if is_sample:
                    o32 = (sakT[hp * 128:(hp + 1) * 128, :], 0, 128, 0, N)
                    proj_rope(s1, s1_lb, rv, wrot_lb, hp * 128, KTn[:, hp, :], [ktn_lb], out32=o32)
                else:
                    o32 = (pakT[hp * 128:(hp + 1) * 128, t0 - 2048:t0 - 2048 + N], 0, 128, 0, N) if (t0 >= 2048 and not cfg.get("skipkout")) else None
                    rb0 = gb0 % NRB
                    proj_rope(s1, s1_lb, rv, wrot_lb, hp * 128, KTs[:, hp, rb0 * 128:rb0 * 128 + N], kt_lb[rb0:rb0 + nb], out32=o32)
            if sub < 3:
                layer_norm(1, N); P.barrier(); return
            s2, s2_lb = ws.get(tiles[2])
            for tb in range(nb):
                i = cnt[0] % 2
                cnt[0] += 1
                pv = ps[:, i, :]
                for k in range(KT):
                    P.op("pe", lambda h, k=k: h.matmul(pv, lhsT=xb[:, k, tb * 128:(tb + 1) * 128], rhs=s2[:, k, :], start=(k == 0), stop=(k == KT - 1)),
                         reads=[s2_lb, xb_lb[k]], writes=[ps_lb[i]], inc=(k == KT - 1))
                if is_sample:
                    P.op("act", lambda h: h.copy(out=Vn[:, 0:512], in_=pv), reads=[ps_lb[i]], writes=[ktn_lb])
                    P.op("act", lambda h: h.activation(out=vstg[:, i, :], in_=pv, func=AF.Copy), reads=[ps_lb[i]], writes=[vstg_lb[i]])
                    P.dma("sp", sav[:, :], vstg[:, i, :], reads=[vstg_lb[i]], sem=f"vo{i}")
                else:
                    gb = gb0 + tb
                    P.op("act", lambda h, gb=gb: h.copy(out=Vst[:, gb % NRB, :], in_=pv), reads=[ps_lb[i]], writes=[v_lb[gb % NRB]])
                    if t0 >= 2048 and not cfg.get("skipvout"):
                        P.op("act", lambda h: h.activation(out=vstg[:, i, :], in_=pv, func=AF.Copy), reads=[ps_lb[i]], writes=[vstg_lb[i]])
                        r0 = t0 - 2048 + tb * 128
                        P.dma("sp", pav[r0:r0 + 128, :], vstg[:, i, :], reads=[vstg_lb[i]], sem=f"vo{i}")
            if sub < 4:
                layer_norm(1, N); P.barrier(); return
            s3, s3_lb = ws.get(tiles[3])
            make_rot(wring[(ws.base + tiles[3]) % NSLOT][:, 0:4096], wrot[:, 0:4096], s3_lb)
            for hp in range(4):
                proj_rope(s3, s3_lb, rv, wrot_lb, hp * 128, QB[:, hp, :], [qb_lb], qz=(QB, hp))
            if sub < 5:
                layer_norm(1, N); P.barrier(); return
            s4, s4_lb = ws.get(tiles[4])
            for kv in range(2):
                for dp in range(2):
                    P.op("dve", lambda h, kv=kv, dp=dp: h.tensor_copy(out=wdup[:, :, kv, dp * 64:(dp + 1) * 64], in_=s4[:, :, kv * 64:(kv + 1) * 64]),
                         reads=[s4_lb], writes=[wdup_lb])
            dsv = wrot[:, 0:2048].rearrange("p (a t i) -> p a t i", t=4, i=32)
            drv = wrot[:, 2048:4096].rearrange("p (a t i) -> p a t i", t=4, i=32)
            for dp in range(2):
                P.op("act", lambda h, dp=dp: h.mul(drv[:, :, 2 * dp, :], dsv[:, :, 2 * dp + 1, :], -1.0), reads=[wdup_lb], writes=[wdup_lb])
                P.op("dve", lambda h, dp=dp: h.tensor_copy(out=drv[:, :, 2 * dp + 1, :], in_=dsv[:, :, 2 * dp, :]), reads=[wdup_lb], writes=[wdup_lb])
            for kv in range(2):
                wz = wdup[:, :, kv, :]
                wr = wdupr[:, :, kv, :]
                if is_sample:
                    proj_rope(wz, wdup_lb, wr, wdup_lb, 0, KTBn[:, kv, :], [ktn_lb], out32=(sbkT[kv * 64:(kv + 1) * 64, :], 0, 64, 0, N))
                else:
                    o32 = (pbkT[kv * 64:(kv + 1) * 64, :], 0, 64, N - 128, N) if gb0 + nb == 32 else None
                    proj_rope(wz, wdup_lb, wr, wdup_lb, 0, KTB[:, kv, (gb0 % 8):(gb0 % 8) + nb, :], ktb_lb[gb0 % 8:(gb0 % 8) + nb], out32=o32)
            for tb in range(nb):
                i = cnt[0] % 2
                cnt[0] += 1
                pv = ps[:, i, 0:128]
                for k in range(KT):
                    P.op("pe", lambda h, k=k: h.matmul(pv, lhsT=xb[:, k, tb * 128:(tb + 1) * 128], rhs=s4[:, k, 128:256], start=(k == 0), stop=(k == KT - 1)),
                         reads=[s4_lb, xb_lb[k]], writes=[ps_lb[i]], inc=(k == KT - 1))
                if is_sample:
                    P.op("act", lambda h: h.copy(out=Vn[:, 512:640], in_=pv), reads=[ps_lb[i]], writes=[ktn_lb])
                    P.op("act", lambda h: h.activation(out=vstg[:, i, 0:128], in_=pv, func=AF.Copy), reads=[ps_lb[i]], writes=[vstg_lb[i]])
                    P.dma("sp", sbv[:, :], vstg[:, i, 0:128], reads=[vstg_lb[i]], sem=f"vo{i}")
                else:
                    gb = gb0 + tb
                    P.op("act", lambda h, gb=gb: h.copy(out=VB[:, gb % 8, :], in_=pv), reads=[ps_lb[i]], writes=[vb_lb[gb % 8]])
                    if gb == 31:
                        P.op("act", lambda h: h.activation(out=vstg[:, i, 0:128], in_=pv, func=AF.Copy), reads=[ps_lb[i]], writes=[vstg_lb[i]])
                        P.dma("sp", pbv[:, :], vstg[:, i, 0:128], reads=[vstg_lb[i]], sem=f"vo{i}")
            def attend(qtile, q_lb, qrow0, qhp, qcols, nq, nheads, key_blocks, obank, dbank, sink_cols):
                W = nheads * nq
                P.op("pe", lambda h: h.matmul(ps[0:64, obank, 0:512], lhsT=zrow[0:1, 0:64], rhs=zrow[0:1, 0:512], start=True, stop=False, skip_group_check=True),
                     reads=[const_lb], writes=[ps_lb[obank]], inc=False)
                P.op("pe", lambda h: h.matmul(ps[0:64, dbank, 0:512], lhsT=zrow[0:1, 0:64], rhs=zrow[0:1, 0:512], start=True, stop=False, skip_group_check=True),
                     reads=[const_lb], writes=[ps_lb[dbank]], inc=True)
                for bi, (kfn, vfn, mask_ap, nk) in enumerate(key_blocks):
                    sbk = 4 + (cnt[0] % 2)
                    pi = cnt[0] % 2
                    cnt[0] += 1
                    for hh in range(nheads):
                        kap, klbs = kfn(hh)
                        P.op("pe", lambda h, kap=kap, hh=hh: h.matmul(ps[0:nk, sbk, hh * nq:(hh + 1) * nq], lhsT=kap,
                                                                      rhs=qtile[:, qhp(hh), qcols[0]:qcols[1]], start=True, stop=True),
                             reads=klbs + [q_lb], writes=[ps_lb[sbk]], inc=(hh == nheads - 1))
                    P.op("act", lambda h: h.activation(out=PT[0:nk, pi, 0:W], in_=ps[0:nk, sbk, 0:W], func=AF.Exp, scale=SC),
                         reads=[ps_lb[sbk]], writes=[pt_lb[pi]])
                    pv3 = PT[0:nk, pi, 0:W].rearrange("p (a b) -> p a b", a=nheads)
                    P.op("dve", lambda h, pv3=pv3, mask_ap=mask_ap: h.tensor_tensor(out=pv3, in0=pv3, in1=mask_ap.unsqueeze(1).to_broadcast([nk, nheads, nq]), op=ALU.mult),
                         reads=[pt_lb[pi], const_lb], writes=[pt_lb[pi]])
                    last = bi == len(key_blocks) - 1
                    for hh in range(nheads):
                        vap, vlbs = vfn(hh)
                        P.op("pe", lambda h, vap=vap, hh=hh: h.matmul(ps[0:64, obank, hh * nq:(hh + 1) * nq], lhsT=vap, rhs=PT[0:nk, pi, hh * nq:(hh + 1) * nq],
                                                                      start=False, stop=last, skip_group_check=True),
                             reads=vlbs + [pt_lb[pi]], writes=[ps_lb[obank]], inc=False)
                    P.op("pe", lambda h: h.matmul(ps[0:64, dbank, 0:W], lhsT=ones_bf[0:nk, 0:64], rhs=PT[0:nk, pi, 0:W], start=False, stop=last, skip_group_check=True),
                         reads=[pt_lb[pi], const_lb], writes=[ps_lb[dbank]], inc=True)
                if sink_cols is not None:
                    for hh in range(nheads):
                        c = sink_cols[0] + hh
                        P.op("dve", lambda h, hh=hh, c=c: h.tensor_scalar(out=rd[:, hh * nq:(hh + 1) * nq], in0=ps[0:64, dbank, hh * nq:(hh + 1) * nq],
                                                                       scalar1=sink_e[0:64, c:c + 1], scalar2=None, op0=ALU.add),
                             reads=[ps_lb[dbank], const_lb], writes=[rd_lb])
                    P.op("dve", lambda h: h.reciprocal(out=rd[:, 0:W], in_=rd[:, 0:W]), reads=[rd_lb], writes=[rd_lb])
                else:
                    P.op("dve", lambda h: h.reciprocal(out=rd[:, 0:W], in_=ps[0:64, dbank, 0:W]), reads=[ps_lb[dbank]], writes=[rd_lb])

            def finalize(head0, nheads, nq, qc):
                for hh in range(nheads):
                    P.op("dve", lambda h, hh=hh: h.tensor_tensor(out=oT[:, head0 + hh, qc[0]:qc[1]], in0=ps[0:64, 6, hh * nq:(hh + 1) * nq],
                                                                 in1=rd[:, hh * nq:(hh + 1) * nq], op=ALU.mult),
                         reads=[ps_lb[6], rd_lb], writes=[ot_lb])

            lvl = cfg.get("attn_level", 9)
            if lvl < 3:
                P.op("dve", lambda h: h.memset(oT[:], 0.0), writes=[ot_lb])
            if not is_sample:
                for qb in range(nb if lvl >= 1 else 0):
                    gq = gb0 + qb
                    qc = (qb * 128, (qb + 1) * 128)
                    for hg in range(2):
                        kbl = []
                        for kb in range(max(0, gq - 16), gq + 1):
                            dl = gq - kb
                            rk = kb % NRB
                            kbl.append((lambda hh, rk=rk, hg=hg: (KTs[:, (4 * hg + hh) // 2, rk * 128:(rk + 1) * 128], [kt_lb[rk]]),
                                        lambda hh, rk=rk, hg=hg: (Vst[:, rk, (4 * hg + hh) * 64:(4 * hg + hh + 1) * 64], [v_lb[rk]]),
                                        mA[:, dl, :], 128))
                        ob, db = 6, 7
                        attend(QA, qa_lb, None, lambda hh, hg=hg: 4 * hg + hh, qc, 128, 4, kbl, ob, db, None)
                        finalize(4 * hg, 4, 128, qc)
                    for kv in range(2 if lvl >= 2 else 0):
                        kbl = []
                        for kb in range(max(0, gq - 1), gq + 1):
                            dl = gq - kb
                            kbl.append((lambda hh, kb=kb, kv=kv: (KTB[:, kv, kb % 8, :], [ktb_lb[kb % 8]]),
                                        lambda hh, kb=kb, kv=kv: (VB[:, kb % 8, kv * 64:(kv + 1) * 64], [vb_lb[kb % 8]]),
                                        mB[:, dl, :], 128))
                        attend(QB, qb_lb, None, lambda hh, kv=kv: 4 * kv + hh, qc, 128, 4, kbl, 6, 7, (4 * kv, 4 * kv + 4))
                        finalize(8 + 4 * kv, 4, 128, qc)
            else:
                def load_seq(sq):
                    par = sq % 2
                    for hf in range(2):
                        P.dma("pool", KTs[:, :, hf * 1024:(hf + 1) * 1024], cakT[sq].rearrange("(hp p) t -> p hp t", p=128)[:, :, hf * 1024:(hf + 1) * 1024],
                              writes=[kt_lb[hf]], sem=f"ck{hf}")
                        P.dma("pool", Vst[:, hf * 8:(hf + 1) * 8, :], cav[sq].rearrange("(b p) n -> p b n", p=128)[:, hf * 8:(hf + 1) * 8, :],
                              writes=[v_lb[hf]], sem=f"cv{hf}")
                    P.dma("pool", KBs[0:64, par, :, :], cbkT[sq].rearrange("(kv d) t -> d kv t", d=64), writes=[kbs_lb[par]], sem=f"cb{par}")
                    P.dma("pool", KBs[64:128, par, :, :], cbkT[sq].rearrange("(kv d) t -> d kv t", d=64), writes=[kbs_lb[par]], sem=f"cb{par}")
                    P.dma("pool", VBs[:, par, :], cbv[sq], writes=[kbs_lb[par]], sem=f"cb{par}")
                    P.dma("sp", Vsq[:, par, :], Vn[sq * 8:(sq + 1) * 8, :], reads=[ktn_lb], writes=[vsq_lb[par]], sem=f"vq{par}")

                load_seq(0)
                for sq in range(16):
                    par = sq % 2
                    qc = (sq * 8, sq * 8 + 8)
                    kbl = []
                    for kb in range(16):
                        kbl.append((lambda hh, kb=kb: (KTs[:, hh // 2, kb * 128:(kb + 1) * 128], [kt_lb[kb // 8]]),
                                    lambda hh, kb=kb: (Vst[:, kb, hh * 64:(hh + 1) * 64], [v_lb[kb // 8]]),
                                    msA[:, kb, :], 128))
                    kbl.append((lambda hh, sq=sq: (KTn[:, hh // 2, sq * 8:sq * 8 + 8], [ktn_lb]),
                                lambda hh, par=par: (Vsq[0:8, par, hh * 64:(hh + 1) * 64], [vsq_lb[par]]),
                                msA[0:8, 16, :], 8))
                    attend(QA, qa_lb, None, lambda hh: hh, qc, 8, 8, kbl, 6, 7, None)
                    if sq + 1 < 16:
                        load_seq(sq + 1)
                    finalize(0, 8, 8, qc)
                    for kv in range(2):
                        kbl = [(lambda hh, kv=kv, par=par: (KBs[:, par, kv, :], [kbs_lb[par]]),
                                lambda hh, kv=kv, par=par: (VBs[:, par, kv * 64:(kv + 1) * 64], [kbs_lb[par]]),
                                msB[:, 0, :], 128),
                               (lambda hh, kv=kv, sq=sq: (KTBn[:, kv, sq * 8:sq * 8 + 8], [ktn_lb]),
                                lambda hh, kv=kv, par=par: (Vsq[0:8, par, 512 + kv * 64:512 + (kv + 1) * 64], [vsq_lb[par]]),
                                msB[0:8, 1, :], 8)]
                        attend(QB, qb_lb, None, lambda hh, kv=kv: 4 * kv + hh, qc, 8, 4, kbl, 6, 7, (4 * kv, 4 * kv + 4))
                        finalize(8 + 4 * kv, 4, 8, qc)

            if sub < 6:
                layer_norm(1, N); P.barrier(); return
            for o in range(KT):
                so, so_lb = ws.get(wo_t[o])
                so = wring[(ws.base + wo_t[o]) % NSLOT][0:64, 0:2048].rearrange("p (a b) -> p a b", a=16)
                b = o % 2
                po = ps[:, b, :N]
                if cfg.get("wo_mode", 9) < 1:
                    continue
                for hh in range(16):
                    P.op("pe", lambda h, hh=hh, so=so: h.matmul(po, lhsT=so[:, hh, :], rhs=oT[:, hh, :], start=(hh == 0), stop=(hh == 15)),
                         reads=[so_lb, ot_lb], writes=[ps_lb[b]], inc=(hh == 15))
                if cfg.get("wo_mode", 9) < 2:
                    continue
                P.op("dve", lambda h, o=o, po=po: h.scalar_tensor_tensor(out=x32[:, o, :N], in0=po, scalar=1.0, in1=x32[:, o, :N],
                                                                         op0=ALU.mult, op1=ALU.add),
                     reads=[ps_lb[b], x32_lb[o]], writes=[x32_lb[o]])
            layer_norm(1, N)
            P.barrier()


    TWO_PI = 2.0 * math.pi
    PI_S = math.pi * (1.0 - 1e-6)
    if stage >= 3:
        s_win = din("s_win", [D, D])
        s_wglu = din("s_wglu", [D, D])
        s_wout = din("s_wout", [D, D])
        s_lre = din("s_lre", [128, 32])
        s_lim = din("s_lim", [128, 32])
        s_ldt = din("s_ldt", [128, 32])
        s_Bre = din("s_Bre", [128, 4096])
        s_Bim = din("s_Bim", [128, 4096])
        s_Cre = din("s_Cre", [128, 4096])
        s_Cim = din("s_Cim", [128, 4096])
        s_d = din("s_d", [128, 8])
        s_bglu = din("s_bglu", [128, 8])
        tau1_d = din("tau1", [128, 512])
        smask_d = din("smask", [128, 128])
        st_re_d = din("st_re", [128, 512])
        st_im_d = din("st_im", [128, 512])
        tw_cr = nc.dram_tensor("tw_cr", [32, 128, 512], F32, kind="Internal").ap()
        tw_si = nc.dram_tensor("tw_si", [32, 128, 512], F32, kind="Internal").ap()
        pc_o = dout("pc", [128, 64])
        sc_o = dout("sc", [128, 1024])
        Bre_sb = sb("Bre_sb", [128, 32, 128], BF16)
        Bim_sb = sb("Bim_sb", [128, 32, 128], BF16)
        Cpr = sb("Cpr", [128, 32, 128], BF16)
        Cni = sb("Cni", [128, 32, 128], BF16)
        rdec = sb("rdec", [128, 32], F32)
        f_re = sb("f_re", [128, 32], F32)
        f_im = sb("f_im", [128, 32], F32)
        car = sb("car", [128, 2, 32], F32)
        crs = sb("crs", [128, 32, 8], F32)
        sis = sb("sis", [128, 32, 8], F32)
        h0t = sb("h0t", [128, 2, 32, 16], F32)
        d_sb = sb("d_sb", [128, 8], F32)
        bglu_sb = sb("bglu_sb", [128, 8], F32)
        smask = sb("smask", [128, 128], F32)
        npi = sb("npi", [128, 1], F32)
        S = LB()
        car_lb = LB()

        def sop(e, fn):
            P.op(e, fn, reads=[S], writes=[S])

        with contextlib.ExitStack() as fs:
            lre = sb("lre", [128, 32], F32, fs)
            lim = sb("lim", [128, 32], F32, fs)
            dtt = sb("dtt", [128, 32], F32, fs)
            thn = sb("thn", [128, 32], F32, fs)
            cs = sb("cs", [128, 32], F32, fs)
            sn = sb("sn", [128, 32], F32, fs)
            a1 = sb("a1", [128, 32], F32, fs)
            a2 = sb("a2", [128, 32], F32, fs)
            a3 = sb("a3", [128, 32], F32, fs)
            a4 = sb("a4", [128, 32], F32, fs)
            tau1 = sb("tau1", [128, 512], F32, fs)
            gu = sb("gu", [128, 512], F32, fs)
            gn = sb("gn", [128, 512], F32, fs)
            gf = sb("gf", [128, 512], F32, fs)
            gm = sb("gm", [128, 512], F32, fs)
            gtab = sb("gtab", [128, 2, 2, 512], F32, fs)
            gtab_lb = [lbs(2), lbs(2)]
            for t_, d_ in ((lre, s_lre), (lim, s_lim), (dtt, s_ldt), (d_sb, s_d), (bglu_sb, s_bglu), (tau1, tau1_d), (smask, smask_d)):
                P.dma("sp", t_[:], d_[:, :], writes=[S], sem="sc")
            P.dma("sp", h0t[:, 0].rearrange("p a b -> p (a b)"), st_re_d[:, :], writes=[S], sem="sc")
            P.dma("sp", h0t[:, 1].rearrange("p a b -> p (a b)"), st_im_d[:, :], writes=[S], sem="sc")
            P.dma("pool", Bre_sb[:].rearrange("p a b -> p (a b)"), s_Bre[:, :], writes=[S], sem="sc2")
            P.dma("pool", Bim_sb[:].rearrange("p a b -> p (a b)"), s_Bim[:, :], writes=[S], sem="sc2")
            sop("dve", lambda h: h.memset(npi[:], -PI_S))
            sop("dve", lambda h: h.memset(car[:], 0.0))
            sop("act", lambda h: h.activation(out=dtt[:], in_=dtt[:], func=AF.Exp))
            sop("dve", lambda h: h.tensor_tensor(out=a1[:], in0=lre[:], in1=dtt[:], op=ALU.mult))
            sop("act", lambda h: h.activation(out=rdec[:], in_=a1[:], func=AF.Exp))
            sop("dve", lambda h: h.tensor_tensor(out=a1[:], in0=lim[:], in1=dtt[:], op=ALU.mult))
            sop("dve", lambda h: h.tensor_scalar(out=thn[:], in0=a1[:], scalar1=1.0 / TWO_PI, scalar2=None, op0=ALU.mult))

            def gen_table(i, shift, dst, dst_lb):
                P.op("dve", lambda h: h.tensor_scalar(out=gu[:], in0=tau1[:], scalar1=thn[:, i:i + 1], scalar2=shift, op0=ALU.mult, op1=ALU.add), reads=[S], writes=[S])
                sop("dve", lambda h: h.tensor_scalar(out=gn[:], in0=gu[:], scalar1=8388608.0, scalar2=None, op0=ALU.add))
                sop("dve", lambda h: h.tensor_scalar(out=gm[:], in0=gn[:], scalar1=8388608.0, scalar2=None, op0=ALU.subtract))
                sop("dve", lambda h: h.tensor_tensor(out=gf[:], in0=gu[:], in1=gm[:], op=ALU.subtract))
                sop("dve", lambda h: h.tensor_scalar(out=gn[:], in0=gf[:], scalar1=0.0, scalar2=None, op0=ALU.is_lt))
                sop("dve", lambda h: h.tensor_tensor(out=gu[:], in0=gf[:], in1=gn[:], op=ALU.add))
                P.op("act", lambda h: h.activation(out=dst, in_=gu[:], func=AF.Sin, scale=TWO_PI * (1.0 - 1e-6), bias=npi[:, 0:1]),
                     reads=[S], writes=[S, dst_lb])

            for i in range(32):
                b = i % 2
                gen_table(i, 0.5, gtab[:, b, 1, :], gtab_lb[b][1])
                gen_table(i, 0.75, gtab[:, b, 0, :], gtab_lb[b][0])
                P.dma("sp", tw_si[i], gtab[:, b, 1, :], reads=[gtab_lb[b][1]], sem=f"tws{b}")
                P.dma("sp", tw_cr[i], gtab[:, b, 0, :], reads=[gtab_lb[b][0]], sem=f"twc{b}")
                P.op("act", lambda h, i=i, b=b: h.copy(out=crs[:, i, :], in_=gtab[:, b, 0, 0:8]), reads=[gtab_lb[b][0]], writes=[S])
                P.op("act", lambda h, i=i, b=b: h.copy(out=sis[:, i, :], in_=gtab[:, b, 1, 0:8]), reads=[gtab_lb[b][1]], writes=[S])
            sop("dve", lambda h: h.tensor_copy(out=cs[:], in_=crs[:, :, 0]))
            sop("dve", lambda h: h.tensor_copy(out=sn[:], in_=sis[:, :, 0]))
            sop("dve", lambda h: h.tensor_tensor(out=a1[:], in0=rdec[:], in1=cs[:], op=ALU.mult))
            sop("dve", lambda h: h.tensor_scalar(out=a1[:], in0=a1[:], scalar1=-1.0, scalar2=None, op0=ALU.add))
            sop("dve", lambda h: h.tensor_tensor(out=a2[:], in0=rdec[:], in1=sn[:], op=ALU.mult))
            sop("dve", lambda h: h.tensor_tensor(out=a3[:], in0=lre[:], in1=lre[:], op=ALU.mult))
            sop("dve", lambda h: h.tensor_tensor(out=a4[:], in0=lim[:], in1=lim[:], op=ALU.mult))
            sop("dve", lambda h: h.tensor_tensor(out=a3[:], in0=a3[:], in1=a4[:], op=ALU.add))
            sop("dve", lambda h: h.reciprocal(out=a3[:], in_=a3[:]))
            sop("dve", lambda h: h.tensor_tensor(out=f_re[:], in0=a1[:], in1=lre[:], op=ALU.mult))
            sop("dve", lambda h: h.tensor_tensor(out=a4[:], in0=a2[:], in1=lim[:], op=ALU.mult))
            sop("dve", lambda h: h.tensor_tensor(out=f_re[:], in0=f_re[:], in1=a4[:], op=ALU.add))
            sop("dve", lambda h: h.tensor_tensor(out=f_re[:], in0=f_re[:], in1=a3[:], op=ALU.mult))
            sop("dve", lambda h: h.tensor_tensor(out=f_im[:], in0=a2[:], in1=lre[:], op=ALU.mult))
            sop("dve", lambda h: h.tensor_tensor(out=a4[:], in0=a1[:], in1=lim[:], op=ALU.mult))
            sop("dve", lambda h: h.tensor_tensor(out=f_im[:], in0=f_im[:], in1=a4[:], op=ALU.subtract))
            sop("dve", lambda h: h.tensor_tensor(out=f_im[:], in0=f_im[:], in1=a3[:], op=ALU.mult))
            c32 = sb("c32", [128, 2, 8, 128], F32, fs)
            ct = sb("ct", [128, 2, 8, 128], F32, fs)
            for ch in range(4):
                i0 = ch * 8
                P.dma("sp", c32[:, 0].rearrange("p a b -> p (a b)"), s_Cre[:, i0 * 128:(i0 + 8) * 128], reads=[S], writes=[S], sem="sc")
                P.dma("sp", c32[:, 1].rearrange("p a b -> p (a b)"), s_Cim[:, i0 * 128:(i0 + 8) * 128], reads=[S], writes=[S], sem="sc")
                frb = f_re[:, i0:i0 + 8].unsqueeze(2).to_broadcast([128, 8, 128])
                fib = f_im[:, i0:i0 + 8].unsqueeze(2).to_broadcast([128, 8, 128])
                sop("dve", lambda h, frb=frb: h.tensor_tensor(out=ct[:, 0], in0=c32[:, 0], in1=frb, op=ALU.mult))
                sop("dve", lambda h, fib=fib: h.tensor_tensor(out=ct[:, 1], in0=c32[:, 1], in1=fib, op=ALU.mult))
                sop("dve", lambda h, i0=i0: h.tensor_tensor(out=Cpr[:, i0:i0 + 8, :], in0=ct[:, 0], in1=ct[:, 1], op=ALU.subtract))
                sop("dve", lambda h, fib=fib: h.tensor_tensor(out=ct[:, 0], in0=c32[:, 0], in1=fib, op=ALU.mult))
                sop("dve", lambda h, frb=frb: h.tensor_tensor(out=ct[:, 1], in0=c32[:, 1], in1=frb, op=ALU.mult))
                sop("dve", lambda h, i0=i0: h.scalar_tensor_tensor(out=Cni[:, i0:i0 + 8, :], in0=ct[:, 0], scalar=-1.0, in1=ct[:, 1], op0=ALU.mult, op1=ALU.subtract))
            sop("dve", lambda h: h.tensor_tensor(out=a1[:], in0=f_re[:], in1=f_re[:], op=ALU.mult))
            sop("dve", lambda h: h.tensor_tensor(out=a2[:], in0=f_im[:], in1=f_im[:], op=ALU.mult))
            sop("dve", lambda h: h.tensor_tensor(out=a1[:], in0=a1[:], in1=a2[:], op=ALU.add))
            sop("dve", lambda h: h.reciprocal(out=a1[:], in_=a1[:]))
            sop("dve", lambda h: h.tensor_tensor(out=a2[:], in0=f_re[:], in1=a1[:], op=ALU.mult))
            sop("dve", lambda h: h.scalar_tensor_tensor(out=a3[:], in0=f_im[:], scalar=-1.0, in1=a1[:], op0=ALU.mult, op1=ALU.mult))
            grb = a2[:, :].unsqueeze(2).to_broadcast([128, 32, 16])
            gib = a3[:, :].unsqueeze(2).to_broadcast([128, 32, 16])
            hh0 = sb("hh0", [128, 4, 32, 16], F32, fs)
            sop("dve", lambda h: h.tensor_tensor(out=hh0[:, 0], in0=h0t[:, 0], in1=grb, op=ALU.mult))
            sop("dve", lambda h: h.tensor_tensor(out=hh0[:, 1], in0=h0t[:, 1], in1=gib, op=ALU.mult))
            sop("dve", lambda h: h.tensor_tensor(out=hh0[:, 2], in0=h0t[:, 0], in1=gib, op=ALU.mult))
            sop("dve", lambda h: h.tensor_tensor(out=hh0[:, 3], in0=h0t[:, 1], in1=grb, op=ALU.mult))
            sop("dve", lambda h: h.tensor_tensor(out=h0t[:, 0], in0=hh0[:, 0], in1=hh0[:, 1], op=ALU.subtract))
            sop("dve", lambda h: h.tensor_tensor(out=h0t[:, 1], in0=hh0[:, 2], in1=hh0[:, 3], op=ALU.add))
            P.barrier()

    def ssm(N, is_sample):
        with contextlib.ExitStack() as fs:
            uT = sb("uT", [128, KT, N], BF16, fs)
            uT_lb = lbs(KT)
            tw = sb("tw", [128, 2, 2, N], F32, fs)
            tw_lb = lbs(2)
            tt = sb("tt", [128, 4, N], F32, fs)
            tt_lb = lbs(4)
            bu = sb("bu", [128, 2, N], F32, fs)
            bu_lb = lbs(2)
            LL = sb("LL", [128, 2, N], F32, fs)
            ll_lb = lbs(2)
            hb = sb("hb", [128, 2, 2, N], BF16, fs)
            hb_lb = lbs(2)
            yb = sb("yb", [128, KT, N], BF16, fs)
            yb_lb = lbs(KT)
            gts = [ttmp[:, 0, :N], ttmp[:, 1, :N], ystage[:, 0, :N]]
            gt_lb = [ttmp_lb[0], ttmp_lb[1], ystage_lb[0]]
            if is_sample:
                rz = sb("rz", [128, 128], F32, fs)
                rz_lb = LB()
                psb = sb("psb", [128, 2, 128], F32, fs)
                psb_lb = lbs(2)
            ws = new_stream()
            mk = lambda dram: [ws.add(dram.rearrange("(k p) n -> p k n", p=128)[:, :, i * 512:(i + 1) * 512], (KT, 512)) for i in range(2)]
            t_in, t_glu, t_out = mk(s_win), mk(s_wglu), mk(s_wout)
            start_stream(ws)
            cnt = [0]

            def linear(tiles, src, src_lb, evac):
                for o in range(KT):
                    wsl, wsl_lb = ws.get(tiles[o // 4])
                    b = cnt[0] % 2
                    cnt[0] += 1
                    po = ps[:, b, :N]
                    for k in range(KT):
                        P.op("pe", lambda h, k=k, wsl=wsl, po=po, o=o: h.matmul(po, lhsT=wsl[:, k, (o % 4) * 128:(o % 4 + 1) * 128], rhs=src[:, k, :N],
                                                                           start=(k == 0), stop=(k == KT - 1)),
                             reads=[wsl_lb, src_lb[k]], writes=[ps_lb[b]], inc=(k == KT - 1))
                    evac(o, po, ps_lb[b])

            linear(t_in, xb, xb_lb, lambda o, po, plb: P.op("act", lambda h: h.copy(out=uT[:, o, :], in_=po), reads=[plb], writes=[uT_lb[o]]))

            for i in range(32):
                j, q = i // 4, i % 4
                hbase, w = 64 * (q // 2), q % 2
                tb = i % 2
                if not is_sample:
                    P.dma("sp", tw[:, tb, 0, :], tw_cr[i], writes=[tw_lb[tb]], sem=f"tl{tb}")
                    P.dma("sp", tw[:, tb, 1, :], tw_si[i], writes=[tw_lb[tb]], sem=f"tl{tb}")
                    tc, ts_ = tw[:, tb, 0, :], tw[:, tb, 1, :]
                    tlb = [tw_lb[tb]]
                    v2 = lambda ap: ap
                else:
                    tc = crs[:, i, :].unsqueeze(1).to_broadcast([128, 16, 8])
                    ts_ = sis[:, i, :].unsqueeze(1).to_broadcast([128, 16, 8])
                    tlb = [S]
                    v2 = lambda ap: ap.rearrange("p (a b) -> p a b", b=8)
                pre, pim = ps[:, 2, :N], ps[:, 3, :N]
                P.op("pe", lambda h: h.matmul(pre, lhsT=Bre_sb[:, i, :], rhs=uT[:, j, :], start=True, stop=True),
                     reads=[S, uT_lb[j]], writes=[ps_lb[2]], inc=True)
                P.op("pe", lambda h: h.matmul(pim, lhsT=Bim_sb[:, i, :], rhs=uT[:, j, :], start=True, stop=True),
                     reads=[S, uT_lb[j]], writes=[ps_lb[3]], inc=True)
                if is_sample:
                    P.op("act", lambda h: h.copy(out=psb[:, 0, :], in_=pre), reads=[ps_lb[2]], writes=[psb_lb[0]])
                    P.op("act", lambda h: h.copy(out=psb[:, 1, :], in_=pim), reads=[ps_lb[3]], writes=[psb_lb[1]])
                    sre, sim_, sre_lb, sim_lb = v2(psb[:, 0, :]), v2(psb[:, 1, :]), psb_lb[0], psb_lb[1]
                else:
                    sre, sim_, sre_lb, sim_lb = pre, pim, ps_lb[2], ps_lb[3]
                P.op("dve", lambda h: h.tensor_tensor(out=v2(tt[:, 0, :]), in0=sre, in1=tc, op=ALU.mult), reads=[sre_lb] + tlb, writes=[tt_lb[0]])
                P.op("dve", lambda h: h.tensor_tensor(out=v2(tt[:, 1, :]), in0=sim_, in1=ts_, op=ALU.mult), reads=[sim_lb] + tlb, writes=[tt_lb[1]])
                P.op("dve", lambda h: h.tensor_tensor(out=v2(tt[:, 2, :]), in0=sim_, in1=tc, op=ALU.mult), reads=[sim_lb] + tlb, writes=[tt_lb[2]])
                P.op("dve", lambda h: h.tensor_tensor(out=v2(tt[:, 3, :]), in0=sre, in1=ts_, op=ALU.mult), reads=[sre_lb] + tlb, writes=[tt_lb[3]])
                P.op("dve", lambda h: h.tensor_tensor(out=bu[:, 0, :], in0=tt[:, 0, :], in1=tt[:, 1, :], op=ALU.add), reads=[tt_lb[0], tt_lb[1]], writes=[bu_lb[0]])
                P.op("dve", lambda h: h.tensor_tensor(out=bu[:, 1, :], in0=tt[:, 2, :], in1=tt[:, 3, :], op=ALU.subtract), reads=[tt_lb[2], tt_lb[3]], writes=[bu_lb[1]])
                if is_sample:
                    P.op("dve", lambda h: h.tensor_scalar(out=rz[:], in0=smask[:], scalar1=rdec[:, i:i + 1], scalar2=None, op0=ALU.mult), reads=[S], writes=[rz_lb])
                    for c in range(2):
                        b0 = v2(bu[:, c, :])[:, :, 0]
                        P.op("dve", lambda h, c=c, b0=b0: h.scalar_tensor_tensor(out=v2(tt[:, c, :])[:, :, 0], in0=h0t[:, c, i, :], scalar=rdec[:, i:i + 1], in1=b0,
                                                                          op0=ALU.mult, op1=ALU.add), reads=[S, bu_lb[c]], writes=[tt_lb[c]])
                        P.op("dve", lambda h, c=c, b0=b0: h.tensor_copy(out=b0, in_=v2(tt[:, c, :])[:, :, 0]), reads=[tt_lb[c]], writes=[bu_lb[c]])
                    for c in range(2):
                        P.op("dve", lambda h, c=c: h.tensor_tensor_scan(out=LL[:, c, :], data0=rz[:], data1=bu[:, c, :], initial=0.0, op0=ALU.mult, op1=ALU.add),
                             reads=[rz_lb, bu_lb[c]], writes=[ll_lb[c]])
                else:
                    for c in range(2):
                        P.op("dve", lambda h, c=c: h.tensor_tensor_scan(out=LL[:, c, :], data0=rdec[:, i:i + 1].to_broadcast([128, N]), data1=bu[:, c, :],
                                                                        initial=car[:, c, i:i + 1], op0=ALU.mult, op1=ALU.add),
                             reads=[S, car_lb, bu_lb[c]], writes=[ll_lb[c]])
                Lr, Li = v2(LL[:, 0, :]), v2(LL[:, 1, :])
                P.op("dve", lambda h: h.tensor_tensor(out=v2(tt[:, 0, :]), in0=Lr, in1=tc, op=ALU.mult), reads=[ll_lb[0]] + tlb, writes=[tt_lb[0]])
                P.op("dve", lambda h: h.tensor_tensor(out=v2(tt[:, 1, :]), in0=Li, in1=ts_, op=ALU.mult), reads=[ll_lb[1]] + tlb, writes=[tt_lb[1]])
                P.op("dve", lambda h: h.tensor_tensor(out=v2(tt[:, 2, :]), in0=Li, in1=tc, op=ALU.mult), reads=[ll_lb[1]] + tlb, writes=[tt_lb[2]])
                P.op("dve", lambda h: h.tensor_tensor(out=v2(tt[:, 3, :]), in0=Lr, in1=ts_, op=ALU.mult), reads=[ll_lb[0]] + tlb, writes=[tt_lb[3]])
                hs = i % 2
                P.op("dve", lambda h: h.tensor_tensor(out=hb[:, hs, 0, :], in0=tt[:, 0, :], in1=tt[:, 1, :], op=ALU.subtract), reads=[tt_lb[0], tt_lb[1]], writes=[hb_lb[hs]])
                P.op("dve", lambda h: h.tensor_tensor(out=hb[:, hs, 1, :], in0=tt[:, 2, :], in1=tt[:, 3, :], op=ALU.add), reads=[tt_lb[2], tt_lb[3]], writes=[hb_lb[hs]])
                if is_sample:
                    P.op("dve", lambda h: h.tensor_tensor(out=h0t[:, 0, i, :], in0=v2(tt[:, 0, :])[:, :, 7], in1=v2(tt[:, 1, :])[:, :, 7], op=ALU.subtract),
                         reads=[tt_lb[0], tt_lb[1], S], writes=[S])
                    P.op("dve", lambda h: h.tensor_tensor(out=h0t[:, 1, i, :], in0=v2(tt[:, 2, :])[:, :, 7], in1=v2(tt[:, 3, :])[:, :, 7], op=ALU.add),
                         reads=[tt_lb[2], tt_lb[3], S], writes=[S])
                else:
                    P.op("dve", lambda h: h.tensor_tensor(out=car[:, 0, i:i + 1], in0=tt[:, 0, N - 1:N], in1=tt[:, 1, N - 1:N], op=ALU.subtract),
                         reads=[tt_lb[0], tt_lb[1], car_lb], writes=[car_lb])
                    P.op("dve", lambda h: h.tensor_tensor(out=car[:, 1, i:i + 1], in0=tt[:, 2, N - 1:N], in1=tt[:, 3, N - 1:N], op=ALU.add),
                         reads=[tt_lb[2], tt_lb[3], car_lb], writes=[car_lb])
                py = ps[:, 4, :N]
                P.op("pe", lambda h: h.matmul(py, lhsT=Cpr[:, i, :], rhs=hb[:, hs, 0, :], start=(q == 0), stop=False),
                     reads=[S, hb_lb[hs]], writes=[ps_lb[4]], inc=False)
                P.op("pe", lambda h: h.matmul(py, lhsT=Cni[:, i, :], rhs=hb[:, hs, 1, :], start=False, stop=(q == 3)),
                     reads=[S, hb_lb[hs]], writes=[ps_lb[4]], inc=True)
                if q == 3:
                    w0, w0_lb = ws.get(t_in[j // 4])
                    pu = ps[:, 5, :N]
                    for k in range(KT):
                        P.op("pe", lambda h, k=k: h.matmul(pu, lhsT=w0[:, k, (j % 4) * 128:(j % 4 + 1) * 128], rhs=xb[:, k, :N], start=(k == 0), stop=(k == KT - 1)),
                             reads=[w0_lb, xb_lb[k]], writes=[ps_lb[5]], inc=(k == KT - 1))
                    P.op("act", lambda h: h.copy(out=gts[0], in_=py), reads=[ps_lb[4]], writes=[gt_lb[0]])
                    P.op("dve", lambda h: h.scalar_tensor_tensor(out=gts[1], in0=pu, scalar=d_sb[:, j:j + 1], in1=gts[0], op0=ALU.mult, op1=ALU.add),
                         reads=[ps_lb[5], gt_lb[0], S], writes=[gt_lb[1]])
                    P.op("act", lambda h: h.activation(out=gts[0], in_=gts[1], func=AF.Square), reads=[gt_lb[1]], writes=[gt_lb[0]])
                    P.op("dve", lambda h: h.tensor_scalar(out=gts[2], in0=gts[0], scalar1=0.044715, scalar2=1.0, op0=ALU.mult, op1=ALU.add),
                         reads=[gt_lb[0]], writes=[gt_lb[2]])
                    P.op("dve", lambda h: h.tensor_tensor(out=gts[0], in0=gts[2], in1=gts[1], op=ALU.mult), reads=[gt_lb[2], gt_lb[1]], writes=[gt_lb[0]])
                    P.op("act", lambda h: h.activation(out=gts[2], in_=gts[0], func=AF.Sigmoid, scale=2.0 * math.sqrt(2.0 / math.pi)),
                         reads=[gt_lb[0]], writes=[gt_lb[2]])
                    P.op("dve", lambda h, j=j: h.tensor_tensor(out=yb[:, j, :], in0=gts[1], in1=gts[2], op=ALU.mult), reads=[gt_lb[1], gt_lb[2]], writes=[yb_lb[j]])

            vb, vb_lb = uT, uT_lb

            def glu_evac(o, po, plb):
                P.op("act", lambda h: h.activation(out=gts[0], in_=po, func=AF.Sigmoid, bias=bglu_sb[:, o:o + 1], scale=1.0), reads=[plb, S], writes=[gt_lb[0]])
                P.op("dve", lambda h: h.tensor_tensor(out=vb[:, o, :], in0=yb[:, o, :], in1=gts[0], op=ALU.mult), reads=[yb_lb[o], gt_lb[0]], writes=[vb_lb[o]])
            linear(t_glu, yb, yb_lb, glu_evac)

            def out_evac(o, po, plb):
                P.op("dve", lambda h: h.scalar_tensor_tensor(out=x32[:, o, :N], in0=po, scalar=1.0, in1=x32[:, o, :N], op0=ALU.mult, op1=ALU.add),
                     reads=[plb, x32_lb[o]], writes=[x32_lb[o]])
            linear(t_out, vb, vb_lb, out_evac)
            layer_norm(4, N)
            P.barrier()

    def ssm_finish():
        with contextlib.ExitStack() as fs:
            o1 = sb("o1", [128, 4, 32], F32, fs)
            po_ = sb("po_", [128, 64], F32, fs)
            sop("dve", lambda h: h.tensor_tensor(out=o1[:, 0], in0=car[:, 0], in1=f_re[:], op=ALU.mult))
            sop("dve", lambda h: h.tensor_tensor(out=o1[:, 1], in0=car[:, 1], in1=f_im[:], op=ALU.mult))
            sop("dve", lambda h: h.tensor_tensor(out=o1[:, 2], in0=car[:, 0], in1=f_im[:], op=ALU.mult))
            sop("dve", lambda h: h.tensor_tensor(out=o1[:, 3], in0=car[:, 1], in1=f_re[:], op=ALU.mult))
            sop("dve", lambda h: h.tensor_tensor(out=po_[:, 0:32], in0=o1[:, 0], in1=o1[:, 1], op=ALU.subtract))
            sop("dve", lambda h: h.tensor_tensor(out=po_[:, 32:64], in0=o1[:, 2], in1=o1[:, 3], op=ALU.add))
            P.dma("sp", pc_o[:, :], po_[:], reads=[S], sem="fo")
            if cfg["sample"]:
                o2 = sb("o2", [128, 4, 32, 16], F32, fs)
                so_ = sb("so_", [128, 2, 32, 16], F32, fs)
                frb = f_re[:, :].unsqueeze(2).to_broadcast([128, 32, 16])
                fib = f_im[:, :].unsqueeze(2).to_broadcast([128, 32, 16])
                sop("dve", lambda h: h.tensor_tensor(out=o2[:, 0], in0=h0t[:, 0], in1=frb, op=ALU.mult))
                sop("dve", lambda h: h.tensor_tensor(out=o2[:, 1], in0=h0t[:, 1], in1=fib, op=ALU.mult))
                sop("dve", lambda h: h.tensor_tensor(out=o2[:, 2], in0=h0t[:, 0], in1=fib, op=ALU.mult))
                sop("dve", lambda h: h.tensor_tensor(out=o2[:, 3], in0=h0t[:, 1], in1=frb, op=ALU.mult))
                sop("dve", lambda h: h.tensor_tensor(out=so_[:, 0], in0=o2[:, 0], in1=o2[:, 1], op=ALU.subtract))
                sop("dve", lambda h: h.tensor_tensor(out=so_[:, 1], in0=o2[:, 2], in1=o2[:, 3], op=ALU.add))
                P.dma("sp", sc_o[:, :], so_[:].rearrange("p a b c -> p (a b c)"), reads=[S], sem="fo")

    def load_unit(src, t0, N):
        v = src.rearrange("(k p) t -> p k t", p=128)
        for k in range(KT):
            q = k % 2
            P.dma("sp", ttmp[:, q, :N], v[:, k, t0:t0 + N], writes=[ttmp_lb[q]], sem=f"xi{q}")
            P.op("act", lambda h, k=k, q=q: h.activation(out=x32[:, k, :N], in_=ttmp[:, q, :N], func=AF.Copy, scale=ALPHA),
                 reads=[ttmp_lb[q]], writes=[x32_lb[k]])
            P.op("dve", lambda h, k=k, q=q: h.tensor_copy(out=xb[:, k, :N], in_=ttmp[:, q, :N]),
                 reads=[ttmp_lb[q]], writes=[xb_lb[k]])

    def run_unit(src, dst, t0, N, is_sample, u):
        load_unit(src, t0, N)
        dv = dst.rearrange("(k p) t -> p k t", p=128)
        fo = lambda k: dv[:, k, t0:t0 + N]
        if stage <= 1:
            ffn(0, 0, N, final_out=fo)
            return
        ffn(0, 0, N)
        if stage == 2:
            attention(N, t0, is_sample)
            ffn(1, 2, N, final_out=fo)
            return
        attention(N, t0, is_sample)
        ffn(1, 2, N)
        ffn(2, 3, N)
        ssm(N, is_sample)
        ffn(3, 5, N, final_out=fo)

    for u in range(n_units):
        run_unit(xpT, ypT, u * UT, UT, False, u)
    if cfg["sample"]:
        run_unit(xsT, ysT, 0, 128, True, 0)
    if stage >= 3:
        ssm_finish()
    P.barrier()
    print("instructions:", P.ninst, {e: P.cnt[e] for e in P.cnt})
    return nc, P


def host_inputs(inputs, c):
    f = lambda a: np.ascontiguousarray(a, dtype=np.float32)
    s = c % 4
    m = {}
    m["xpT"] = f(inputs["x_prompt"][s].T)
    m["xsT"] = f(inputs["x_sample"][16 * c:16 * c + 16].reshape(128, D).T)
    m["wg"] = f(inputs["ffn_w_gate"].reshape(4, D, DFF))
    m["wu"] = f(inputs["ffn_w_up"].reshape(4, D, DFF))
    m["wd"] = f(inputs["ffn_w_down"].reshape(4, DFF, D))
    m["w_in"] = f(inputs["attn_w_in"][0])
    m["w_out"] = f(inputs["attn_w_out"][0])
    inv = 10000.0 ** (-np.arange(32, dtype=np.float32) / 32)
    fr = np.tile(inv, 4)[:, None].astype(np.float32)
    pos = np.arange(SEQ, dtype=np.float32)[None, :]
    m["ropec"] = np.cos(fr * pos).astype(np.float32)
    m["ropes"] = np.sin(fr * pos).astype(np.float32)
    poss = np.tile(16384.0 + np.arange(8, dtype=np.float32), 16)[None, :]
    m["ropecs"] = np.cos(fr * poss).astype(np.float32)
    m["ropess"] = np.sin(fr * poss).astype(np.float32)

    def mult(dl):
        dl = np.asarray(dl)
        return ((dl >= 0) & (dl <= 128)).astype(np.float32) + ((dl >= 0) & (dl <= 512) & (dl % 4 == 0)) + ((dl >= 0) & (dl <= 2048) & (dl % 16 == 0))
    j = np.arange(128)[:, None, None]
    i = np.arange(128)[None, None, :]
    b = np.arange(17)[None, :, None]
    m["mA"] = f(mult(128 * b + i - j).reshape(128, 17 * 128))
    dB = 128 * np.arange(2)[None, :, None] + i - j
    m["mB"] = f(((dB >= 0) & (dB <= 127)).reshape(128, 2 * 128))
    i8 = np.arange(8)[None, None, :]
    ms = mult(2048 + i8 - (128 * b + j)).astype(np.float32)
    ms[:, 16, :] = 0
    ms[:8, 16, :] = mult(i8[0] - np.arange(8)[:, None])
    m["msA"] = f(ms.reshape(128, 17 * 8))
    mb = np.zeros((128, 2, 8), np.float32)
    d0 = 128 + i8[0] - np.arange(128)[:, None]
    mb[:, 0, :] = (d0 >= 0) & (d0 <= 127)
    d1 = i8[0] - np.arange(8)[:, None]
    mb[:8, 1, :] = (d1 >= 0)
    m["msB"] = f(mb.reshape(128, 16))
    m["sinks"] = f(np.broadcast_to(inputs["attn_sinks"][0][None, :], (128, 8)))
    sl = slice(16 * c, 16 * c + 16)
    m["cakT"] = f(inputs["cache_a_k"][0, sl].reshape(16, 2048, 512).transpose(0, 2, 1))
    m["cav"] = f(inputs["cache_a_v"][0, sl].reshape(16, 2048, 512))
    m["cbkT"] = f(inputs["cache_b_k"][0, sl].reshape(16, 128, 128).transpose(0, 2, 1))
    m["cbv"] = f(inputs["cache_b_v"][0, sl].reshape(16, 128, 128))
    m["s_win"] = f(inputs["ssm_w_in"][0])
    m["s_wglu"] = f(inputs["ssm_w_glu"][0])
    m["s_wout"] = f(inputs["ssm_w_out"][0])
    st = lambda a: f(np.asarray(a).reshape(32, 2, 64).transpose(1, 2, 0).reshape(128, 32))
    m["s_lre"] = st(inputs["ssm_lambda_re"][0])
    m["s_lim"] = st(inputs["ssm_lambda_im"][0])
    m["s_ldt"] = st(np.broadcast_to(inputs["ssm_log_dt"][0][:, None], (64, 64)))
    def btab(b):
        t = np.zeros((128, 32, 128), np.float32)
        for i in range(32):
            for gg in range(2):
                r0 = 32 * (i % 4) + 16 * gg
                t[r0:r0 + 16, i, 64 * gg:64 * gg + 64] = b[2 * i + gg].T
        return t.reshape(128, 4096)
    def ctab(cc):
        t = np.zeros((128, 32, 128), np.float32)
        for i in range(32):
            for gg in range(2):
                c0 = 32 * (i % 4) + 16 * gg
                t[64 * gg:64 * gg + 64, i, c0:c0 + 16] = cc[2 * i + gg].T
        return t.reshape(128, 4096)
    m["s_Bre"] = btab(inputs["ssm_b_re"][0]); m["s_Bim"] = btab(inputs["ssm_b_im"][0])
    m["s_Cre"] = ctab(inputs["ssm_c_re"][0]); m["s_Cim"] = ctab(inputs["ssm_c_im"][0])
    m["s_d"] = f(inputs["ssm_d"][0].reshape(8, 128).T)
    m["s_bglu"] = f(inputs["ssm_b_glu"][0].reshape(8, 128).T)
    m["tau1"] = f(np.broadcast_to(np.arange(1, 513, dtype=np.float32)[None, :], (128, 512)))
    sm = np.ones((128, 128), np.float32); sm[:, ::8] = 0
    m["smask"] = sm
    stt = lambda a: f(np.asarray(a).reshape(16, 32, 2, 64).transpose(2, 3, 1, 0).reshape(128, 512))
    m["st_re"] = stt(inputs["state_c_re"][0, 16 * c:16 * c + 16]); m["st_im"] = stt(inputs["state_c_im"][0, 16 * c:16 * c + 16])
    m["lng"] = f(inputs["ln_g"].reshape(6, KT, 128).transpose(2, 0, 1).reshape(128, 48))
    m["lnb"] = f(inputs["ln_b"].reshape(6, KT, 128).transpose(2, 0, 1).reshape(128, 48))
    return m


def run(inputs, cfg=None, trace=False):
    cfg = dict(CFG) if cfg is None else cfg
    nc, P = build(cfg)
    shared = {}
    in_maps = []
    for c in range(8):
        m = host_inputs(inputs, c)
        for k in ("wg", "wu", "wd", "lng", "lnb", "w_in", "w_out", "ropec", "ropes", "ropecs", "ropess", "mA", "mB", "msA", "msB", "sinks",
                  "s_win", "s_wglu", "s_wout", "s_lre", "s_lim", "s_ldt", "s_Bre", "s_Bim", "s_Cre", "s_Cim", "s_d", "s_bglu", "tau1", "smask"):
            if k in shared:
                m[k] = shared[k]
            else:
                shared[k] = m[k]
        in_maps.append(m)
    if not cfg["sample"]:
        in_maps = [{k: v for k, v in m.items() if k not in ("cakT", "cav", "cbkT", "cbv")} for m in in_maps]
    if cfg["stage"] < 3:
        in_maps = [{k: v for k, v in m.items() if not (k.startswith("s_") or k in ("tau1", "smask", "st_re", "st_im"))} for m in in_maps]
    res = run_bass_kernel_spmd(nc, in_maps, core_ids=list(range(8)), trace=trace)
    return res


def kernel(**inputs):
    inputs = {k: np.asarray(v) for k, v in inputs.items()}
    res = run(inputs)
    r = res.results
    yp = np.stack([r[s]["ypT"].T for s in range(4)], 0)
    ys = np.concatenate([r[c]["ysT"].T.reshape(16, 8, D) for c in range(8)], 0)
    pak = np.stack([r[s]["pakT"].T.reshape(2048, 8, 64) for s in range(4)], 0)[None]
    pav = np.stack([r[s]["pav"].reshape(2048, 8, 64) for s in range(4)], 0)[None]
    pbk = np.stack([r[s]["pbkT"].T.reshape(128, 2, 64) for s in range(4)], 0)[None]
    pbv = np.stack([r[s]["pbv"].reshape(128, 2, 64) for s in range(4)], 0)[None]
    sak = np.concatenate([r[c]["sakT"].T.reshape(16, 8, 8, 64) for c in range(8)], 0)[None]
    sav = np.concatenate([r[c]["sav"].reshape(16, 8, 8, 64) for c in range(8)], 0)[None]
    sbk = np.concatenate([r[c]["sbkT"].T.reshape(16, 8, 2, 64) for c in range(8)], 0)[None]
    sbv = np.concatenate([r[c]["sbv"].reshape(16, 8, 2, 64) for c in range(8)], 0)[None]
    unst = lambda a: a.reshape(2, 64, 32).transpose(2, 0, 1).reshape(64, 64)
    pcr = np.stack([unst(r[s]["pc"][:, 0:32]) for s in range(4)], 0)[None]
    pci = np.stack([unst(r[s]["pc"][:, 32:64]) for s in range(4)], 0)[None]
    unss = lambda a: a.reshape(2, 64, 32, 16).transpose(3, 2, 0, 1).reshape(16, 64, 64)
    scr = np.concatenate([unss(r[c]["sc"][:, 0:512]) for c in range(8)], 0)[None]
    sci = np.concatenate([unss(r[c]["sc"][:, 512:1024]) for c in range(8)], 0)[None]
    out = (yp, ys, pak, pav, pbk, pbv, pcr, pci, sak, sav, sbk, sbv, scr, sci)
    return tuple(np.ascontiguousarray(o, dtype=np.float32) for o in out)
```
